# Optimizing a Trainium2 kernel written in Bass

```python
import math
import jax, jax.numpy as jnp
from jax import lax
import numpy as np

D_MODEL = 1024
BATCH = 32
SEQ = 256
DEPTH = 2
DEC_BATCH = 2
DEC_SEQ = 4096
PAST_LEN = 256

GRID_W = 64
D_A = 512
CONV_K = 31
CONV_PAD = (CONV_K - 1) // 2
H_B = 4
DK_B = 128
DV_B = 128
D_B = H_B * DV_B
CHUNK = 32
H_C = 4
DH_C = 64
D_C = H_C * 2 * DH_C
QBLK = 128
ROPE_BASE = 10000.0
D_FF = ((8 * D_MODEL + 3 * 256 - 1) // (3 * 256)) * 256
SPLIT_SIZES = (2 * D_A, D_B, D_B, D_B, D_B, D_B, D_C, D_C, D_C, 3 * D_MODEL)
IN_COLS = 2 * D_A + 5 * D_B + 3 * D_C + 3 * D_MODEL
N_BRANCH = 3

kernel_name = 'hybrid_diffusion_conv_hgrn2_diffattn_step'


def rmsnorm(x, g, eps=1e-6):
    xf = x.astype(jnp.float32)
    y = xf * lax.rsqrt(jnp.mean(xf * xf, axis=-1, keepdims=True) + eps)
    return (y * g.astype(jnp.float32)).astype(x.dtype)


def layernorm(x, g, b, eps=1e-5):
    xf = x.astype(jnp.float32)
    mu = jnp.mean(xf, axis=-1, keepdims=True)
    xc = xf - mu
    y = xc * lax.rsqrt(jnp.mean(xc * xc, axis=-1, keepdims=True) + eps)
    return (y * g.astype(jnp.float32) + b.astype(jnp.float32)).astype(x.dtype)


def axial_rope(n):
    rows = n // GRID_W
    row = jnp.broadcast_to(jnp.arange(rows)[:, None], (rows, GRID_W)).reshape(-1).astype(jnp.float32)
    col = jnp.broadcast_to(jnp.arange(GRID_W)[None, :], (rows, GRID_W)).reshape(-1).astype(jnp.float32)
    half = DH_C // 2
    inv = ROPE_BASE ** (-jnp.arange(0, half, 2, dtype=jnp.float32) / half)
    ang = jnp.concatenate([row[:, None] * inv, col[:, None] * inv], axis=-1)
    return jnp.cos(ang), jnp.sin(ang)


def apply_rope(x, cos, sin):
    c = cos[None, :, None, None, :].astype(x.dtype)
    s = sin[None, :, None, None, :].astype(x.dtype)
    x1 = x[..., 0::2]
    x2 = x[..., 1::2]
    return jnp.stack([x1 * c - x2 * s, x1 * s + x2 * c], axis=-1).reshape(x.shape)


def gla_chunk_scan(q, k, v, logf, s0):
    B, N, H, _ = q.shape
    DV = v.shape[-1]
    n = N // CHUNK

    def blocks(t):
        return t.reshape(B, n, CHUNK, H, t.shape[-1]).transpose(1, 0, 3, 2, 4)

    causal = jnp.tril(jnp.ones((CHUNK, CHUNK), dtype=bool))[:, :, None]

    def step(S, inp):
        qc, kc, vc, gc = inp
        b = jnp.cumsum(gc, axis=2)
        o_inter = jnp.einsum('bhtd,bhde->bhte', qc * jnp.exp(b), S)
        diff = b[:, :, :, None, :] - b[:, :, None, :, :]
        decay = jnp.where(causal, jnp.exp(jnp.where(causal, diff, 0.0)), 0.0)
        a = jnp.einsum('bhtsd,bhsd->bhts', qc[:, :, :, None, :] * decay, kc)
        o_intra = jnp.einsum('bhts,bhse->bhte', a, vc)
        b_last = b[:, :, -1:, :]
        S = jnp.exp(b_last[:, :, 0, :])[..., None] * S + jnp.einsum('bhsd,bhse->bhde', kc * jnp.exp(b_last - b), vc)
        return S, o_inter + o_intra

    s_fin, o = lax.scan(step, s0, (blocks(q), blocks(k), blocks(v), blocks(logf)))
    o = o.transpose(1, 0, 3, 2, 4).reshape(B, N, H, DV)
    return o, s_fin


def diff_attention(q, k, v, lam):
    B, N, H, _, DH = q.shape
    nb = N // QBLK
    qb = jnp.moveaxis(q.reshape(B, nb, QBLK, H, 2, DH), 1, 0)
    scale = DH ** -0.5

    def one_block(qblk):
        s = jnp.einsum('bqhcd,bkhcd->bhcqk', qblk, k, preferred_element_type=jnp.float32) * scale
        p = jax.nn.softmax(s, axis=-1)
        w = (p[:, :, 0] - lam * p[:, :, 1]).astype(v.dtype)
        return jnp.einsum('bhqk,bkhe->bqhe', w, v)

    o = lax.map(one_block, qb)
    return jnp.moveaxis(o, 0, 1).reshape(B, N, H, v.shape[-1])


def trunk_layer(x, cvec, l, rope, ctx, params):
    (w_mod, b_mod, norm1, norm2, w_in, conv_w, conv_b, conv_ln_g, conv_ln_b, hgrn_lb, hgrn_norm,
     q_norm, k_norm, lambda_qk, subln, w_branch, w_out, w_ffn_in, w_ffn_out) = params
    B, N, _ = x.shape
    f32 = jnp.float32

    mod = jnp.dot(jax.nn.silu(cvec), w_mod[l]) + b_mod[l]
    sh1, sc1, g1, sh2, sc2, g2 = jnp.split(mod[:, None, :], 6, axis=-1)

    h = rmsnorm(x, norm1[l]) * (1 + sc1) + sh1
    u = h @ w_in[l]
    idx = np.cumsum(SPLIT_SIZES)[:-1].tolist()
    a_glu, hq, hi, hf_f, hf_b, hg, aq, ak, av, gates = jnp.split(u, idx, axis=-1)

    a = a_glu[..., :D_A] * jax.nn.sigmoid(a_glu[..., D_A:])
    a = lax.conv_general_dilated(a, conv_w[l][:, None, :], (1,), [(CONV_PAD, CONV_PAD)],
                                 dimension_numbers=('NWC', 'WIO', 'NWC'), feature_group_count=D_A) + conv_b[l]
    a = jax.nn.silu(layernorm(a, conv_ln_g[l], conv_ln_b[l]))
    y_a = a @ w_branch[l, :D_A]

    sm = jax.nn.softmax(hgrn_lb.astype(f32), axis=0)
    lb = (jnp.cumsum(sm, axis=0)[l] - sm[0]).reshape(2, 1, 1, H_B, DK_B)
    z = jnp.stack([hf_f, hf_b], axis=0).astype(f32).reshape(2, B, N, H_B, DK_B)
    fgate = lb + (1.0 - lb) * jax.nn.sigmoid(z)
    kgate = 1.0 - fgate
    logf = jnp.log(fgate)
    qb_ = jax.nn.silu(hq.astype(f32)).reshape(B, N, H_B, DK_B)
    vb_ = hi.astype(f32).reshape(B, N, H_B, DV_B)
    if ctx is None:
        s0 = jnp.zeros((B, 2, H_B, DK_B, DV_B), f32)
    else:
        s0 = jnp.broadcast_to(ctx[2].astype(f32), (B, 2, H_B, DK_B, DV_B))
    o_f, s_f = gla_chunk_scan(qb_, kgate[0], vb_, logf[0], s0[:, 0])
    fl = lambda t: jnp.flip(t, axis=1)
    o_b, s_b = gla_chunk_scan(fl(qb_), fl(kgate[1]), fl(vb_), fl(logf[1]), s0[:, 1])
    o_hg = rmsnorm(o_f + fl(o_b), hgrn_norm[l]) * jax.nn.silu(hg.astype(f32).reshape(B, N, H_B, DV_B))
    y_b = o_hg.reshape(B, N, D_B).astype(x.dtype) @ w_branch[l, D_A:D_A + D_B]
    new_s = jnp.stack([s_f, s_b], axis=1).astype(x.dtype)

    q = rmsnorm(aq.reshape(B, N, H_C, 2, DH_C), q_norm[l])
    k = rmsnorm(ak.reshape(B, N, H_C, 2, DH_C), k_norm[l])
    v = av.reshape(B, N, H_C, 2 * DH_C)
    if rope is not None:
        q = apply_rope(q, rope[0], rope[1])
        k = apply_rope(k, rope[0], rope[1])
    if ctx is None:
        k_all, v_all = k, v
    else:
        k_all = jnp.concatenate([ctx[0].astype(k.dtype), k], axis=1)
        v_all = jnp.concatenate([ctx[1].astype(v.dtype), v], axis=1)
    lam_init = 0.8 - 0.6 * math.exp(-0.3 * l)
    lq = lambda_qk[l].astype(f32)
    lam = jnp.exp(jnp.sum(lq[0] * lq[1])) - jnp.exp(jnp.sum(lq[2] * lq[3])) + lam_init
    o_c = diff_attention(q, k_all, v_all, lam)
    o_c = rmsnorm(o_c, subln[l]) * (1.0 - lam_init)
    y_c = o_c.reshape(B, N, D_C) @ w_branch[l, D_A + D_B:]

    s = jax.nn.sigmoid(gates.reshape(B, N, N_BRANCH, D_MODEL))
    m = s[:, :, 0] * y_a + s[:, :, 1] * y_b + s[:, :, 2] * y_c
    x = x + g1 * (m @ w_out[l])

    h2 = rmsnorm(x, norm2[l]) * (1 + sc2) + sh2
    gu = h2 @ w_ffn_in[l]
    ff = jax.nn.silu(gu[..., :D_FF]) * gu[..., D_FF:]
    x = x + g2 * (ff @ w_ffn_out[l])
    return x, k, v, new_s


def setup_inputs(seed: int = 0) -> dict:
    key = jax.random.key(seed)
    ks = jax.random.split(key, 32)
    nrm = lambda i, shape, sc: jax.random.normal(ks[i], shape, jnp.float32) * sc
    D = D_MODEL
    return {
        'x_prompt': nrm(0, (BATCH, SEQ, D), 1.0),
        'x_sample': nrm(1, (DEC_BATCH, DEC_SEQ, D), 1.0),
        'cache_k': nrm(2, (DEC_BATCH, DEPTH, PAST_LEN, H_C, 2, DH_C), 1.0),
        'cache_v': nrm(3, (DEC_BATCH, DEPTH, PAST_LEN, H_C, 2 * DH_C), 1.0),
        'state_hgrn': nrm(4, (DEC_BATCH, DEPTH, 2, H_B, DK_B, DV_B), 0.5),
        'c': nrm(5, (DEC_BATCH, D), 1.0),
        'c_ctx': nrm(6, (D,), 1.0),
        'w_mod': nrm(7, (DEPTH, D, 6 * D), 0.5 * D ** -0.5),
        'b_mod': nrm(8, (DEPTH, 6 * D), 0.02),
        'norm1': 1.0 + nrm(9, (DEPTH, D), 0.05),
        'norm2': 1.0 + nrm(10, (DEPTH, D), 0.05),
        'w_in': nrm(11, (DEPTH, D, IN_COLS), D ** -0.5),
        'conv_w': nrm(12, (DEPTH, CONV_K, D_A), CONV_K ** -0.5),
        'conv_b': nrm(13, (DEPTH, D_A), 0.02),
        'conv_ln_g': 1.0 + nrm(14, (DEPTH, D_A), 0.05),
        'conv_ln_b': nrm(15, (DEPTH, D_A), 0.02),
        'hgrn_lb': nrm(16, (DEPTH, 2, D_B), 0.1),
        'hgrn_norm': 1.0 + nrm(17, (DEPTH, DV_B), 0.05),
        'q_norm': 1.0 + nrm(18, (DEPTH, DH_C), 0.05),
        'k_norm': 1.0 + nrm(19, (DEPTH, DH_C), 0.05),
        'lambda_qk': nrm(20, (DEPTH, 4, DH_C), 0.1),
        'subln': 1.0 + nrm(21, (DEPTH, 2 * DH_C), 0.05),
        'w_branch': nrm(22, (DEPTH, D_A + D_B + D_C, D), 512 ** -0.5),
        'w_out': nrm(23, (DEPTH, D, D), D ** -0.5),
        'w_ffn_in': nrm(24, (DEPTH, D, 2 * D_FF), D ** -0.5),
        'w_ffn_out': nrm(25, (DEPTH, D_FF, D), D_FF ** -0.5),
    }


def reference(x_prompt, x_sample, cache_k, cache_v, state_hgrn, c, c_ctx, w_mod, b_mod, norm1, norm2, w_in,
              conv_w, conv_b, conv_ln_g, conv_ln_b, hgrn_lb, hgrn_norm, q_norm, k_norm, lambda_qk, subln,
              w_branch, w_out, w_ffn_in, w_ffn_out):
    params = (w_mod, b_mod, norm1, norm2, w_in, conv_w, conv_b, conv_ln_g, conv_ln_b, hgrn_lb, hgrn_norm,
              q_norm, k_norm, lambda_qk, subln, w_branch, w_out, w_ffn_in, w_ffn_out)

    xp = x_prompt
    cvec_ctx = c_ctx[None, :]
    ks, vs, ss = [], [], []
    for l in range(DEPTH):
        xp, k_l, v_l, s_l = trunk_layer(xp, cvec_ctx, l, None, None, params)
        ks.append(k_l)
        vs.append(v_l)
        ss.append(s_l)
    new_cache_k = jnp.stack(ks, axis=1)
    new_cache_v = jnp.stack(vs, axis=1)
    new_state_hgrn = jnp.stack(ss, axis=1)

    xs = x_sample
    rope = axial_rope(xs.shape[1])
    for l in range(DEPTH):
        ctx = (cache_k[:, l], cache_v[:, l], state_hgrn[:, l])
        xs, _, _, _ = trunk_layer(xs, c, l, rope, ctx, params)

    return (xp, xs, new_cache_k, new_cache_v, new_state_hgrn)
```

```python
import math
from contextlib import ExitStack

import numpy as np
import concourse.bass as bass
import concourse.mybir as mybir
from concourse.bass_utils import run_bass_kernel_spmd

F32 = mybir.dt.float32
BF16 = mybir.dt.bfloat16
AF = mybir.ActivationFunctionType
ALU = mybir.AluOpType
AX = mybir.AxisListType

D = 1024
KC = 8
NL = 2
HC = 1280
C_CA, C_CG, C_HQ, C_HFF, C_HFB, C_HG, C_AQ, C_AK, C_HI, C_AV = range(10)
DFF = 2816
EPS_RMS = 1e-6
EPS_LN = 1e-5
GROUPS = [[0, 1, 2, 3], [4, 5, 6, 7]]
import os as _os
LITE = set(_os.environ.get("MK_LITE", "").split(",")) - {""}


class T:
    __slots__ = ("ap", "name", "lastw", "readers", "excl")

    def __init__(self, ap, name="", excl=False):
        self.ap = ap
        self.name = name
        self.lastw = None
        self.readers = []
        self.excl = excl

    def __getitem__(self, idx):
        return self.ap[idx]


class Op:
    __slots__ = ("eng", "fn", "deps", "tok", "signal", "is_dma", "multi", "inc", "clock")

    def __init__(self, eng, fn, is_dma, multi, inc):
        self.eng = eng
        self.fn = fn
        self.deps = []
        self.tok = None
        self.signal = False
        self.is_dma = is_dma
        self.multi = multi
        self.inc = inc
        self.clock = None


ENGS = ("pe", "act", "dve", "pool", "sp")
EIDX = {e: i for i, e in enumerate(ENGS)}


class Sched:
    def __init__(self, nc, n_dma_sems=14):
        self.nc = nc
        self.ops = {e: [] for e in ENGS}
        self.n_dma_sems = n_dma_sems
        self.final = []
        self.pending_fence = {e: [] for e in ENGS}
        self.last = {e: None for e in ENGS}
        self.open_dmas = []

    def _dep(self, op, prod):
        if prod is None or prod is op:
            return
        op.deps.append(prod)
        prod.signal = True

    def op(self, eng, fn, reads=(), writes=(), dma=False, multi=False, inc=None):
        o = Op(eng, fn, dma, multi, inc if inc is not None else (16 if dma else 1))
        for t in reads:
            self._dep(o, t.lastw)
            if t.excl:
                for r in t.readers:
                    if r.eng != eng:
                        self._dep(o, r)
        for t in writes:
            lw = t.lastw
            if lw is not None and not (lw.eng == eng == "pe"):
                self._dep(o, lw)
            for r in t.readers:
                self._dep(o, r)
        if self.pending_fence[eng]:
            for p in self.pending_fence[eng]:
                self._dep(o, p)
            self.pending_fence[eng] = []
        for t in reads:
            t.readers.append(o)
        for t in writes:
            t.lastw = o
            t.readers = []
        self.ops[eng].append(o)
        if dma:
            if o.inc != 1:
                self.open_dmas.append(o)
        else:
            self.last[eng] = o
        return o

    def dma(self, eng, out, in_, reads=(), writes=()):
        return self.op(eng, lambda e: e.dma_start(out=out, in_=in_), reads, writes, dma=True)

    def barrier(self):
        deps = [self.last[e] for e in ENGS if self.last[e] is not None]
        deps += self.open_dmas
        self.open_dmas = []
        for e in ENGS:
            self.pending_fence[e] = self.pending_fence[e] + list(deps)

    def finish(self, op):
        op.signal = True
        self.final.append(op)

    def emit(self):
        nc = self.nc
        with ExitStack() as es:
            sems = {e: es.enter_context(nc.semaphore("s_" + e)) for e in ENGS}
            dsems = {}
            for e in ("act", "pool", "sp"):
                for i in range(self.n_dma_sems):
                    dsems[(e, i)] = es.enter_context(nc.semaphore("d_%s%d" % (e, i)))
            csem = es.enter_context(nc.semaphore("s_cc"))
            allsem = dict(dsems)
            allsem.update(sems)
            allsem["cc"] = csem
            dcount, dprev = {}, {}
            ccount = 0
            for e in ENGS:
                cnt, nd = 0, 0
                for o in self.ops[e]:
                    if o.is_dma:
                        if o.inc == 1:
                            ccount += 1
                            o.tok = ("cc", ccount)
                            continue
                        k = (e, nd % self.n_dma_sems)
                        nd += 1
                        dcount[k] = dcount.get(k, 0) + 16
                        o.tok = (k, dcount[k])
                        if k in dprev:
                            o.deps.append(dprev[k])
                        dprev[k] = o
                    elif o.signal:
                        cnt += 1
                        o.tok = (e, cnt)
            stats = {"waits": 0, "ops": 0}
            block = es.enter_context(nc.Block())

            def run(e, eng):
                seen = {}
                for o in self.ops[e]:
                    need = {}
                    for d in o.deps:
                        k, v = d.tok
                        if seen.get(k, 0) < v and need.get(k, 0) < v:
                            need[k] = v
                    items = list(need.items())
                    attach = None
                    if items and not o.multi:
                        attach = items[0]
                        items = items[1:]
                    for k, v in items:
                        eng.wait_ge(allsem[k], v)
                        stats["waits"] += 1
                    ins = o.fn(eng)
                    stats["ops"] += 1
                    if attach is not None:
                        ins._wait_ge(allsem[attach[0]], attach[1])
                    for k, v in need.items():
                        seen[k] = v
                    if o.tok is not None:
                        ins.then_inc(allsem[o.tok[0]], o.inc)
                if e == "sp":
                    for o in self.final:
                        k, v = o.tok
                        if seen.get(k, 0) < v:
                            eng.wait_ge(allsem[k], v)
                            seen[k] = v

            @block.tensor
            def _(eng):
                run("pe", eng)

            @block.scalar
            def _(eng):
                run("act", eng)

            @block.vector
            def _(eng):
                run("dve", eng)

            @block.gpsimd
            def _(eng):
                run("pool", eng)

            @block.sync
            def _(eng):
                run("sp", eng)
            self.stats = stats


class Prog:
    def __init__(self, stop_after=None, taps=()):
        self.nc = bass.Bass("TRN2", target_bir_lowering=False)
        self.S = Sched(self.nc)
        self.stop_after = stop_after
        self.taps = taps
        import os
        self.skip = set(os.environ.get("MK_SKIP", "").split(","))
        self.dbg = os.environ.get("MK_DBG") == "1"
        self.cut = int(os.environ.get("MK_CUT", "99"))
        self.units = int(os.environ.get("MK_UNITS", "99"))
        self.bank_i = 0
        self.uid = 0

    def din(self, name, shape, dt=F32):
        if name in LITE:
            shape = [1, 1]
        return self.nc.dram_tensor(name, list(shape), dt, kind="ExternalInput").ap()

    def dout(self, name, shape, dt=F32):
        return self.nc.dram_tensor(name, list(shape), dt, kind="ExternalOutput").ap()

    def dint(self, name, shape, dt):
        return self.nc.dram_tensor(name, list(shape), dt).ap()

    def sb(self, es, name, shape, dt):
        self.uid += 1
        return es.enter_context(self.nc.sbuf_tensor("%s_%d" % (name, self.uid), list(shape), dt))[:]

    def sbt(self, es, name, shape, dt):
        return T(self.sb(es, name, shape, dt), name)

    def bank(self):
        b = self.PS[self.bank_i % 8]
        self.bank_i += 1
        return b

    def mm(self, out, lhsT, rhs, start, stop, reads, writes, tile_position=None):
        kw = {}
        if tile_position is not None:
            kw["tile_position"] = tile_position
        return self.S.op("pe", lambda e: e.matmul(out, lhsT=lhsT, rhs=rhs, start=start, stop=stop, **kw), reads, writes)

    def tr(self, out, in_, reads, writes):
        ident = self.identF
        return self.S.op("pe", lambda e: e.transpose(out=out, in_=in_, identity=ident[:]), list(reads) + [ident], writes)

    def act(self, out, in_, func, reads, writes, scale=1.0, bias=0.0, accum_out=None):
        if accum_out is not None:
            return self.S.op("act", lambda e: e.activation(out=out, in_=in_, func=func, scale=scale, bias=bias, accum_out=accum_out),
                             reads, writes, multi=True)
        return self.S.op("act", lambda e: e.activation(out=out, in_=in_, func=func, scale=scale, bias=bias), reads, writes)

    def tt(self, out, in0, in1, op, reads, writes, eng="dve"):
        return self.S.op(eng, lambda e: e.tensor_tensor(out=out, in0=in0, in1=in1, op=op), reads, writes)

    def ts(self, out, in0, s1, s2, op0, op1, reads, writes, eng="dve"):
        if op1 is None:
            return self.S.op(eng, lambda e: e.tensor_scalar(out=out, in0=in0, scalar1=s1, scalar2=None, op0=op0), reads, writes)
        return self.S.op(eng, lambda e: e.tensor_scalar(out=out, in0=in0, scalar1=s1, scalar2=s2, op0=op0, op1=op1), reads, writes)

    def stt(self, out, in0, scalar, in1, op0, op1, reads, writes):
        return self.S.op("dve", lambda e: e.scalar_tensor_tensor(out=out, in0=in0, scalar=scalar, in1=in1, op0=op0, op1=op1), reads, writes)

    def cp(self, out, in_, reads, writes, eng="dve"):
        if eng == "act":
            return self.S.op("act", lambda e: e.activation(out=out, in_=in_, func=AF.Identity), reads, writes)
        return self.S.op(eng, lambda e: e.tensor_copy(out=out, in_=in_), reads, writes)

    def memset(self, ap, val, writes, eng="pool"):
        return self.S.op(eng, lambda e: e.memset(ap, val), (), writes)

    def dma(self, out, in_, reads=(), writes=(), q="sp"):
        return self.S.dma(q, out, in_, reads, writes)

    def build(self):
        nc, S = self.nc, self.S
        I = {}
        I["xp"] = self.din("xp", [1024, D])
        I["xs"] = self.din("xs", [1024, D])
        I["cvecT"] = self.din("cvecT", [128, KC, 2])
        I["w_mod"] = self.din("w_mod", [NL, D, 6 * D])
        I["bmodT"] = self.din("bmodT", [128, NL, 48])
        I["bmodg"] = self.din("bmodg", [128, NL, 2, D])
        I["n1T"] = self.din("n1T", [128, NL, KC])
        I["n2T"] = self.din("n2T", [128, NL, KC])
        I["w_hm"] = self.din("w_hm", [NL, 4, D, HC])
        I["w_hd"] = self.din("w_hd", [NL, D, HC])
        I["w_gate"] = self.din("w_gate", [NL, D, 3 * D])
        I["cwT"] = self.din("cwT", [128, NL, 4, 31])
        I["cwT_hd"] = self.din("cwT_hd", [128, NL, 31])
        I["cb"] = self.din("cb", [128, NL, 4])
        I["cb_hd"] = self.din("cb_hd", [128, NL])
        I["lng"] = self.din("lng", [128, NL, 4])
        I["lnb"] = self.din("lnb", [128, NL, 4])
        I["lbT"] = self.din("lbT", [128, NL, 2, 4])
        I["lbT_hd"] = self.din("lbT_hd", [128, NL, 2])
        I["hnw"] = self.din("hnw", [128, NL])
        I["qnw"] = self.din("qnw", [128, NL])
        I["knw"] = self.din("knw", [128, NL])
        I["subw"] = self.din("subw", [128, NL])
        I["lqbc"] = self.din("lqbc", [128, NL, 4, 64])
        I["w_branch"] = self.din("w_branch", [NL, 1536, D])
        I["w_out"] = self.din("w_out", [NL, D, D])
        I["w_ffn_in"] = self.din("w_ffn_in", [NL, D, 2 * DFF])
        I["w_ffn_out"] = self.din("w_ffn_out", [NL, DFF, D])
        I["ck_hd"] = self.din("ck_hd", [NL, 256, 128])
        I["cv_hd"] = self.din("cv_hd", [NL, 256, 128])
        I["st_hd"] = self.din("st_hd", [NL, 2, 128, 128])
        I["identF"] = self.din("identF", [128, 128])
        I["maskF"] = self.din("maskF", [128, 128])
        I["maskB"] = self.din("maskB", [128, 128])
        I["bd64"] = self.din("bd64", [128, 128])
        I["ropeR"] = self.din("ropeR", [128, 128])
        I["ind4"] = self.din("ind4", [128, 4])
        I["cosT"] = self.din("cosT", [128, 4096])
        I["sinT"] = self.din("sinT", [128, 4096])
        O = {}
        O["yp"] = self.dout("yp", [1024, D])
        O["ys"] = self.dout("ys", [1024, D])
        O["nk"] = self.dout("nk", [4, NL, 256, 512])
        O["nv"] = self.dout("nv", [4, NL, 256, 512])
        O["ns"] = self.dout("ns", [4, NL, 2, 4, 128, 128])
        self.I, self.O = I, O
        self.ag1_in = [self.dint("ag1_in%d" % i, [512, 1024], BF16) for i in range(2)]
        self.ag1_out = [self.dint("ag1_out%d" % i, [4 * 512, 1024], BF16) for i in range(2)]
        self.ag2_in = [self.dint("ag2_in%d" % i, [128, 4096], BF16) for i in range(3)]
        self.ag2_out = [self.dint("ag2_out%d" % i, [4 * 128, 4096], BF16) for i in range(3)]
        self.t_ag1_in = [T(a, "ag1_in") for a in self.ag1_in]
        self.t_ag1_out = [T(a, "ag1_out") for a in self.ag1_out]
        self.t_ag2_in = [T(a, "ag2_in") for a in self.ag2_in]
        self.t_ag2_out = [T(a, "ag2_out") for a in self.ag2_out]
        self.t_yp = T(O["yp"], "yp")
        self.t_ys = T(O["ys"], "ys")
        self.outs_pending = []
        self.taps_out = {}

        with ExitStack() as es:
            self.es = es
            self._anon = T(None, "anon")
            self.PS = [T(es.enter_context(nc.psum_tensor("ps%d" % i, [128, 512], F32))[:], "ps%d" % i, excl=True) for i in range(8)]
            sbt = lambda n, s, d: self.sbt(es, n, s, d)
            self.identF = sbt("identF", [128, 128], F32)
            self.maskF = sbt("maskF", [128, 128], F32)
            self.maskB = sbt("maskB", [128, 128], F32)
            self.bd64 = sbt("bd64", [128, 128], F32)
            self.ropeR = sbt("ropeR", [128, 128], F32)
            self.ind4 = sbt("ind4", [128, 4], F32)
            self.onesF = sbt("onesF", [128, 128], F32)
            self.onesB = sbt("onesB", [128, 128], BF16)
            self.reset = sbt("reset", [128, 512], F32)
            self.epsc = {EPS_RMS: sbt("eps_rms", [128, 1], F32), EPS_LN: sbt("eps_ln", [128, 1], F32)}
            for nm in ("identF", "maskF", "maskB", "bd64", "ropeR", "ind4"):
                t = getattr(self, nm)
                self.dma(t[:], I[nm][:], writes=[t])
            self.memset(self.onesF[:], 1.0, [self.onesF])
            self.memset(self.onesB[:], 1.0, [self.onesB])
            self.memset(self.reset[:], 1.0, [self.reset])
            rv = self.reset.ap.rearrange("p (c t) -> p c t", t=32)
            self.memset(rv[:, :, 0:1], 0.0, [self.reset])
            self.memset(self.epsc[EPS_RMS][:], EPS_RMS, [self.epsc[EPS_RMS]])
            self.memset(self.epsc[EPS_LN][:], EPS_LN, [self.epsc[EPS_LN]])
            sp = {}
            for nm, shp in (("cvecT", [128, KC, 2]), ("bmodT", [128, NL, 48]), ("n1T", [128, NL, KC]), ("n2T", [128, NL, KC]),
                            ("cwT", [128, NL, 4, 31]), ("cwT_hd", [128, NL, 31]), ("cb", [128, NL, 4]), ("cb_hd", [128, NL]),
                            ("lng", [128, NL, 4]), ("lnb", [128, NL, 4]), ("lbT", [128, NL, 2, 4]), ("lbT_hd", [128, NL, 2]),
                            ("hnw", [128, NL]), ("qnw", [128, NL]), ("knw", [128, NL]), ("subw", [128, NL]),
                            ("lqbc", [128, NL, 4, 64])):
                t = sbt(nm, shp, F32)
                self.dma(t[:], I[nm][:], writes=[t])
                sp[nm] = t
            self.sp = sp
            self.HT = sbt("HT", [128, KC, 1024], BF16)
            self.MX = sbt("MX", [128, 12, 1024], BF16)
            self.modT = sbt("modT", [128, 48, 2], F32)
            self.A1 = sbt("A1", [128, KC, 2], F32)
            self.A2 = sbt("A2", [128, KC, 2], F32)
            self.SC = sbt("SC", [128, KC, 2], BF16)
            self.SCR = sbt("SCR", [128, KC, 2, 128], BF16)
            self.neglam = sbt("neglam", [128, NL], F32)
            self.subw2 = sbt("subw2", [128, NL], F32)
            self.lbv = sbt("lbv", [128, NL, 2, 4], F32)
            self.omlv = sbt("omlv", [128, NL, 2, 4], F32)
            self.nomlv = sbt("nomlv", [128, NL, 2, 4], F32)
            self.lbv_hd = sbt("lbv_hd", [128, NL, 2], F32)
            self.omlv_hd = sbt("omlv_hd", [128, NL, 2], F32)
            self.nomlv_hd = sbt("nomlv_hd", [128, NL, 2], F32)
            self.setup_small()
            S.barrier()
            phases = []
            for l in range(NL):
                phases += [("mod", l), ("snorm", l), ("pmix", l), ("merge0", l), ("smix", l), ("ffn0", l), ("merge1", l), ("ffn1", l)]
            for pi_, (ph, l) in enumerate(phases):
                if self.stop_after is not None and pi_ >= self.stop_after:
                    break
                if ph in self.skip:
                    continue
                if ph == "mod":
                    self.mod_phase(l)
                elif ph == "snorm":
                    self.sample_norm(l)
                elif ph == "pmix":
                    self.prompt_mixers(l)
                elif ph == "merge0":
                    self.merge_phase(l, 0)
                elif ph == "ffn0":
                    self.ffn_phase(l, 0)
                elif ph == "smix":
                    self.sample_mixers(l)
                elif ph == "merge1":
                    self.merge_phase(l, 1)
                elif ph == "ffn1":
                    self.ffn_phase(l, 1)
                S.barrier()
            for o in self.outs_pending:
                S.finish(o)
            S.emit()
        return nc

    def setup_small(self):
        sp = self.sp
        with ExitStack() as es:
            t1 = self.sbt(es, "ss_t1", [128, 64], F32)
            s01 = self.sbt(es, "ss_s01", [128, 2], F32)
            e01 = self.sbt(es, "ss_e01", [128, 2], F32)
            for l in range(NL):
                lam_init = 0.8 - 0.6 * math.exp(-0.3 * l)
                for j in range(2):
                    self.tt(t1[:], sp["lqbc"][:, l, 2 * j, :], sp["lqbc"][:, l, 2 * j + 1, :], ALU.mult, [sp["lqbc"]], [t1])
                    self.S.op("dve", lambda e, j=j: e.tensor_reduce(out=s01[:, j:j + 1], in_=t1[:], axis=AX.X, op=ALU.add), [t1], [s01])
                self.act(e01[:], s01[:], AF.Exp, [s01], [e01])
                self.tt(self.neglam[:, l:l + 1], e01[:, 1:2], e01[:, 0:1], ALU.subtract, [e01], [self.neglam])
                self.ts(self.neglam[:, l:l + 1], self.neglam[:, l:l + 1], -lam_init, None, ALU.add, None, [self.neglam], [self.neglam])
                self.ts(self.subw2[:, l:l + 1], sp["subw"][:, l:l + 1], 1.0 - lam_init, None, ALU.mult, None, [sp["subw"]], [self.subw2])
            for (src, lb, oml, noml) in ((sp["lbT"], self.lbv, self.omlv, self.nomlv), (sp["lbT_hd"], self.lbv_hd, self.omlv_hd, self.nomlv_hd)):
                self.memset(lb[:, 0], 0.0, [lb], eng="dve")
                self.tt(lb[:, 1], src[:, 1], src[:, 0], ALU.subtract, [src], [lb])
                self.act(lb[:, 1], lb[:, 1], AF.Sigmoid, [lb], [lb])
                self.ts(oml[:], lb[:], -1.0, 1.0, ALU.mult, ALU.add, [lb], [oml])
                self.ts(noml[:], oml[:], -1.0, None, ALU.mult, None, [oml], [noml])
            self.act(self.SC[:], sp["cvecT"][:], AF.Silu, [sp["cvecT"]], [self.SC])
            self.cp(self.SCR[:], self.SC.ap.unsqueeze(3).to_broadcast([128, KC, 2, 128]), [self.SC], [self.SCR])

    def load_w512(self, slot, src_rows_ap, ncols=512):
        self.dma(slot[:, :, 0:ncols], src_rows_ap.rearrange("(k p) c -> p k c", p=128), writes=[slot], q="pool")

    def mod_phase(self, l):
        I, sp = self.I, self.sp
        with ExitStack() as es:
            slots = [self.sbt(es, "wm%d" % i, [128, KC, 512], BF16) for i in range(3)]
            bank = self.bank()
            for blk in range(12):
                sl = slots[blk % 3]
                self.load_w512(sl, I["w_mod"][l, :, blk * 512:(blk + 1) * 512])
                for c4 in range(4):
                    ch = blk * 4 + c4
                    for kc in range(KC):
                        self.mm(bank[:, 2 * ch:2 * ch + 2], sl[:, kc, c4 * 128:(c4 + 1) * 128], self.SC[:, kc, :], kc == 0, kc == KC - 1,
                                [sl, self.SC], [bank])
            pv = bank.ap[:, 0:96].rearrange("p (c t) -> p c t", t=2)
            self.tt(self.modT[:], pv, sp["bmodT"][:, l, :].unsqueeze(2).to_broadcast([128, 48, 2]), ALU.add, [bank, sp["bmodT"]], [self.modT])
            for (A, nT, c0) in ((self.A1, sp["n1T"], 8), (self.A2, sp["n2T"], 32)):
                self.ts(A[:], self.modT[:, c0:c0 + KC, :], 1.0, None, ALU.add, None, [self.modT], [A])
                self.tt(A[:], A[:], nT[:, l, :].unsqueeze(2).to_broadcast([128, KC, 2]), ALU.mult, [A, nT], [A])

    def norm_hT(self, X, A, B, ci, es, dst_fn):
        ss = self.sbt(es, "n_ss", [128, 8], F32)
        lnv = self.sbt(es, "n_ln", [128, 8], F32)
        rstd = self.sbt(es, "n_rstd", [128, 8], F32)
        junk = self.sbt(es, "n_junk", [128, 1024], F32)
        xs = [self.sbt(es, "n_xs%d" % i, [128, 1024], F32) for i in range(2)]
        for t in range(8):
            self.act(junk[:], X[:, t, :], AF.Square, [X], [junk, ss], accum_out=ss[:, t:t + 1])
        self.act(lnv[:], ss[:], AF.Ln, [ss], [lnv], scale=1.0 / D, bias=self.epsc[EPS_RMS][:, 0:1])
        self.act(rstd[:], lnv[:], AF.Exp, [lnv], [rstd], scale=-0.5)
        for t in range(8):
            x_ = xs[t % 2]
            self.ts(x_[:], X[:, t, :], rstd[:, t:t + 1], None, ALU.mult, None, [X, rstd], [x_])
            for half in range(2):
                bank = self.bank()
                for k4 in range(4):
                    kc = half * 4 + k4
                    self.tr(bank[:, k4 * 128:(k4 + 1) * 128], x_[:, kc * 128:(kc + 1) * 128], [x_], [bank])
                for k4 in range(4):
                    kc = half * 4 + k4
                    dap, dt_ = dst_fn(kc, t)
                    self.act(dap, bank[:, k4 * 128:(k4 + 1) * 128], AF.Identity, [bank, A, B], [dt_],
                             scale=A[:, kc, ci:ci + 1], bias=B[:, kc, ci:ci + 1])

    def allgather(self, ag_in, ag_out, t_in, t_out):
        self.S.op("pool", lambda e: e.collective_compute("AllGather", ALU.bypass, replica_groups=GROUPS,
                                                        ins=[ag_in[:, :]], outs=[ag_out[:, :]]),
                  [t_in], [t_out], dma=True, inc=1)

    def load_X(self, X, src_ap):
        self.dma(X[:], src_ap.rearrange("(t p) d -> p t d", p=128), writes=[X])

    def sample_norm(self, l):
        with ExitStack() as es:
            X = self.sbt(es, "X", [128, 8, 1024], F32)
            src = self.I["xs"] if l == 0 else self.O["ys"]
            self.S.op("sp", lambda e: e.dma_start(out=X[:], in_=src.rearrange("(t p) d -> p t d", p=128)),
                      [self.t_ys] if l else [], [X], dma=True)
            B1 = T(self.modT.ap[:, 0:KC, :], "B1")
            self.norm_hT(X, self.A1, self.modT, 1, es, lambda kc, t: (self.HT[:, kc, t * 128:(t + 1) * 128], self.HT))
            for hf in range(2):
                self.dma(self.ag1_in[hf].rearrange("(k p) t -> p k t", p=128), self.HT[:, 4 * hf:4 * hf + 4, :], [self.HT], [self.t_ag1_in[hf]])
                self.allgather(self.ag1_in[hf], self.ag1_out[hf], self.t_ag1_in[hf], self.t_ag1_out[hf])

    def prompt_mixers(self, l):
        I, O, sp = self.I, self.O, self.sp
        with ExitStack() as es:
            with ExitStack() as es1:
                X = self.sbt(es1, "X", [128, 8, 1024], F32)
                src = I["xp"] if l == 0 else O["yp"]
                self.S.op("sp", lambda e: e.dma_start(out=X[:], in_=src.rearrange("(t p) d -> p t d", p=128)),
                          [self.t_yp] if l else [], [X], dma=True)
                self.norm_hT(X, self.A1, self.modT, 0, es1, lambda kc, t: (self.HT[:, kc, t * 128:(t + 1) * 128], self.HT))
            self.S.barrier()
            WH = [self.sbt(es, "WH%d" % i, [128, KC, HC], BF16) for i in range(2)]
            ucache = [{}, {}]
            nkst = [self.sbt(es, "nkst%d" % i, [128, 2, 128], F32) for i in range(2)]
            nvst = [self.sbt(es, "nvst%d" % i, [128, 2, 128], F32) for i in range(2)]
            ui = 0
            prev_g = None
            for h in range(4):
                W = WH[h % 2]
                self.dma(W[:], I["w_hm"][l, h].rearrange("(k p) c -> p k c", p=128), writes=[W], q="pool")
                for s in range(4):
                    cfg = dict(
                        l=l, L=256, BT=256, W=W, rope=False, past=0,
                        get_hT=lambda j, s=s: (self.HT.ap[:, :, s * 256:(s + 1) * 256], self.HT),
                        cw=sp["cwT"][:, l, h, :], cw_t=sp["cwT"], cb=sp["cb"][:, l, h:h + 1], cb_t=sp["cb"],
                        lb=[self.lbv[:, l, d, h:h + 1] for d in range(2)], oml=[self.omlv[:, l, d, h:h + 1] for d in range(2)],
                        noml=[self.nomlv[:, l, d, h:h + 1] for d in range(2)], lb_t=[self.lbv, self.omlv, self.nomlv],
                        sink=lambda idx, j, s=s, h=h: (self.MX[:, 4 * idx + h, s * 256:(s + 1) * 256], [self.MX], None),
                        state_in=None,
                        state_out=lambda d, s=s, h=h: O["ns"][s, l, d, h],
                        nk_out=(O["nk"][s, l, :, h * 128:(h + 1) * 128], nkst[ui % 2]),
                        nv_out=(O["nv"][s, l, :, h * 128:(h + 1) * 128], nvst[ui % 2]),
                        ctx=None, es=es, cache=ucache[ui % 2],
                    )
                    ui += 1
                    if ui <= self.units:
                        g = self.seq_unit(cfg)
                        cur_at_ab, prev_done = False, prev_g is None
                        while not (cur_at_ab and prev_done):
                            if not cur_at_ab:
                                r = next(g, "END")
                                if r == "AB" or r == "END":
                                    cur_at_ab = True
                            if not prev_done:
                                r = next(prev_g, "END")
                                if r == "END":
                                    prev_done = True
                        prev_g = g
            if prev_g is not None:
                for _ in prev_g:
                    pass

    def sample_mixers(self, l):
        I, O, sp = self.I, self.O, self.sp
        with ExitStack() as es:
            W = self.sbt(es, "WHs", [128, KC, HC], BF16)
            self.dma(W[:], I["w_hd"][l].rearrange("(k p) c -> p k c", p=128), writes=[W], q="pool")
            hb = [T(self.HT.ap[:, :, i * 512:(i + 1) * 512], "hb%d" % i) for i in range(2)]
            stg = [self.sbt(es, "sk%d" % i, [128, 512], BF16) for i in range(4)]
            cnt = {"hb": 0, "stg": 0}
            ag1 = self.ag1_out

            def get_hT(j):
                t = hb[cnt["hb"] % 2]
                cnt["hb"] += 1
                rr, half = j // 2, j % 2
                for hf in range(2):
                    self.dma(t.ap[:, 4 * hf:4 * hf + 4, :],
                             ag1[hf][rr * 512:(rr + 1) * 512, half * 512:(half + 1) * 512].rearrange("(k p) t -> p k t", p=128),
                             [self.t_ag1_out[hf]], [t, self.HT])
                return (t.ap, t)

            def sink(idx, j):
                t = stg[cnt["stg"] % 4]
                cnt["stg"] += 1
                dst = self.ag2_in[idx][:, j * 512:(j + 1) * 512]
                return (t[:], [t], lambda: self.dma(dst, t[:], [t], [self.t_ag2_in[idx]]))

            cfg = dict(
                l=l, L=4096, BT=512, W=W, rope=True, past=256, get_hT=get_hT,
                cw=sp["cwT_hd"][:, l, :], cw_t=sp["cwT_hd"], cb=sp["cb_hd"][:, l:l + 1], cb_t=sp["cb_hd"],
                lb=[self.lbv_hd[:, l, d:d + 1] for d in range(2)], oml=[self.omlv_hd[:, l, d:d + 1] for d in range(2)],
                noml=[self.nomlv_hd[:, l, d:d + 1] for d in range(2)], lb_t=[self.lbv_hd, self.omlv_hd, self.nomlv_hd],
                sink=sink, state_in=lambda d: I["st_hd"][l, d], state_out=None, nk_out=None, nv_out=None, alias_mx=True,
                ctx=(I["ck_hd"][l], I["cv_hd"][l]),
            )
            for _ in self.seq_unit(cfg):
                pass
            for i3 in range(3):
                self.allgather(self.ag2_in[i3], self.ag2_out[i3], self.t_ag2_in[i3], self.t_ag2_out[i3])
            if self.dbg and l == 0:
                for i3 in range(3):
                    d_ = self.dout("dbg_ag2_%d" % i3, [512, 4096], BF16)
                    o = self.dma(d_[:, :], self.ag2_out[i3][:, :], [self.t_ag2_out[i3]], [])
                    self.outs_pending.append(o)
                for hf in range(2):
                    d_ = self.dout("dbg_ag1_%d" % hf, [2048, 1024], BF16)
                    o = self.dma(d_[:, :], self.ag1_out[hf][:, :], [self.t_ag1_out[hf]], [])
                    self.outs_pending.append(o)

    def proj_fm(self, hTb, hT_t, W, c, BT):
        bank = self.bank()
        for kc in range(KC):
            self.mm(bank[:, 0:BT], W[:, kc, c * 128:(c + 1) * 128], hTb[:, kc, 0:BT], kc == 0, kc == KC - 1, [W, hT_t], [bank])
        return bank

    def seq_unit(self, cfg):
        S = self.S
        l, L, BT, W = cfg["l"], cfg["L"], cfg["BT"], cfg["W"]
        NB = L // BT
        NT = BT // 128
        NCH = BT // 32
        past = cfg["past"]
        NK = (past + L) // 128
        sp = self.sp
        with ExitStack() as es_local:
            cache = cfg.get("cache")
            es = cfg["es"] if cache is not None else es_local

            def mk(n, s_, d):
                if cache is None:
                    return self.sbt(es, n, s_, d)
                if n not in cache:
                    cache[n] = self.sbt(es, n, s_, d)
                return cache[n]
            APAD = mk("APAD", [128, L + 30], BF16)
            if cache is not None and "apad_t" in cache:
                apad_t = cache["apad_t"]
            else:
                apad_t = [T(APAD.ap, "apad_e")] + [T(APAD.ap, "apad%d" % j) for j in range(NB)]
                if cache is not None:
                    cache["apad_t"] = apad_t
            QH = [mk("QH%d" % j, [128, BT], BF16) for j in range(NB)]
            GH = [mk("GH%d" % j, [128, BT], BF16) for j in range(NB)]
            VH = [mk("VH%d" % j, [128, NT, 128], BF16) for j in range(NB)]
            if cfg.get("alias_mx"):
                mxb = self.MX.ap.rearrange("p a b -> p (a b)")
                mxf = mxb.bitcast(F32)
                SGB = [T(mxf[:, j * BT:(j + 1) * BT], "SGB%d" % j) for j in range(NB)]
                QT = [T(mxb[:, 8192 + j * BT:8192 + (j + 1) * BT], "QT%d" % j) for j in range(NB)]
            else:
                SGB = [mk("SGB%d" % j, [128, BT], F32) for j in range(NB)]
                QT = [mk("QT%d" % j, [128, BT], BF16) for j in range(NB)]
            OF = [mk("OF%d" % j, [128, BT], BF16) for j in range(NB)]
            QTb = [mk("QTb%d" % j, [128, BT], BF16) for j in range(NB)]
            for j in range(NB):
                self.memset(QT[j][64:128, :], 0.0, [QT[j]], eng="pool")
                self.memset(QTb[j][0:64, :], 0.0, [QTb[j]], eng="pool")
            VB = [mk("VB%d" % i, [128, 4, 128], BF16) for i in range(2)]
            KTp = mk("KTp", [128, max(past, 32)], BF16)
            VAp = mk("VAp", [128, max(past // 128, 1), 128], BF16)
            KT = [mk("KT%d" % j, [128, BT], BF16) for j in range(NB)]
            VA = [mk("VA%d" % j, [128, NT, 128], BF16) for j in range(NB)]
            DG = mk("DG", [128, 31, 128], BF16)
            S32r = [[mk("S32_%d_%d" % (d, i), [128, 128], F32) for i in range(4)] for d in range(2)]
            si = [0, 0]
            S16A = [mk("S16A%d" % i, [128, NCH, 128], BF16) for i in range(2)]
            AMA = [mk("AMA%d" % i, [128, NT, 128], BF16) for i in range(2)]
            NTMP = 8
            tmp = [mk("tmp%d" % i, [128, BT], F32) for i in range(NTMP)]
            ti = [0]

            def tmpf():
                t = tmp[ti[0] % NTMP]
                ti[0] += 1
                return t
            QD = [mk("QD%d" % i, [128, BT], BF16) for i in range(2)]
            KD = [mk("KD%d" % i, [128, BT], BF16) for i in range(2)]
            KL = [mk("KL%d" % i, [128, NT, 128], BF16) for i in range(2)]
            DV = [mk("DV%d" % i, [128, NCH], F32) for i in range(2)]
            AM = [mk("AM%d" % i, [128, 128], BF16) for i in range(2)]
            PB = [mk("PB%d" % i, [128, BT], BF16) for i in range(4)]
            pb_i = [0]
            vb_i = [0]
            cs = None
            if cfg["rope"]:
                cs = [(mk("cos%d" % i, [128, BT], F32), mk("sin%d" % i, [128, BT], F32)) for i in range(2)]

            self.memset(APAD[:, 0:15], 0.0, [apad_t[0]], eng="dve")
            self.memset(APAD[:, L + 15:L + 30], 0.0, [apad_t[0]], eng="dve")
            self.tt(DG[:], self.identF.ap.unsqueeze(1).to_broadcast([128, 31, 128]),
                    cfg["cw"].unsqueeze(2).to_broadcast([128, 31, 128]), ALU.mult, [self.identF, cfg["cw_t"]], [DG])

            for d in range(2):
                if cfg["state_in"] is None:
                    self.memset(S32r[d][0][:], 0.0, [S32r[d][0]], eng="dve")
                else:
                    self.dma(S32r[d][0][:], cfg["state_in"](d), writes=[S32r[d][0]])
            if cfg["ctx"] is not None:
                ck, cv = cfg["ctx"]
                ckt = mk("ckt", [128, past // 128, 128], F32)
                self.dma(ckt[:], ck.rearrange("(t p) d -> p t d", p=128), writes=[ckt])
                self.dma(VAp[:], cv.rearrange("(t p) d -> p t d", p=128), writes=[VAp], q="pool")
                bank = self.bank()
                for t_ in range(past // 128):
                    self.tr(bank[:, t_ * 128:(t_ + 1) * 128], ckt[:, t_, :], [ckt], [bank])
                self.cp(KTp[:, 0:past], bank[:, 0:past], [bank], [KTp], eng="act")

            lbt = cfg["lb_t"]

            def dir_prep(d, j, sig_ap, sig_reads, slot):
                logf, kT, pc = tmpf(), tmpf(), tmpf()
                self.act(logf[:], sig_ap, AF.Ln, sig_reads + lbt, [logf], scale=cfg["oml"][d], bias=cfg["lb"][d])
                self.ts(kT[:], sig_ap, cfg["noml"][d], cfg["oml"][d], ALU.mult, ALU.add, sig_reads + lbt, [kT])
                rs = self.reset
                self.S.op("dve", lambda e: e.tensor_tensor_scan(out=pc[:], data0=rs[:, 0:BT], data1=logf[:], initial=0.0,
                                                                op0=ALU.mult, op1=ALU.add), [rs, logf], [pc])
                pc3 = pc.ap.rearrange("p (c t) -> p c t", t=32)
                tot = pc3[:, :, 31:32]
                totb = tot.to_broadcast([128, NCH, 32])
                if d == 0:
                    b = pc
                else:
                    b = tmpf()
                    b3 = b.ap.rearrange("p (c t) -> p c t", t=32)
                    self.tt(b3, totb, pc3, ALU.subtract, [pc], [b])
                    self.tt(b[:], b[:], logf[:], ALU.add, [b, logf], [b])
                b3 = b.ap.rearrange("p (c t) -> p c t", t=32)
                eb, enb, dd = tmpf(), tmpf(), tmpf()
                self.act(eb[:], b[:], AF.Exp, [b], [eb])
                self.tt(QD[slot][:], QH[j][:], eb[:], ALU.mult, [QH[j], eb], [QD[slot]])
                self.act(enb[:], b[:], AF.Exp, [b], [enb], scale=-1.0)
                self.tt(KD[slot][:], kT[:], enb[:], ALU.mult, [kT, enb], [KD[slot]])
                dd3 = dd.ap.rearrange("p (c t) -> p c t", t=32)
                self.tt(dd3, totb, b3, ALU.subtract, [pc, b], [dd])
                self.act(dd[:], dd[:], AF.Exp, [dd], [dd])
                self.tt(dd[:], kT[:], dd[:], ALU.mult, [kT, dd], [dd])
                self.act(DV[slot].ap.unsqueeze(2), tot, AF.Exp, [pc], [DV[slot]])
                bank = self.bank()
                for t_ in range(NT):
                    self.tr(bank[:, t_ * 128:(t_ + 1) * 128], dd[:, t_ * 128:(t_ + 1) * 128], [dd], [bank])
                self.cp(KL[slot].ap.rearrange("p t d -> p (t d)"), bank[:, 0:BT], [bank], [KL[slot]], eng="act")

            def scan_a(d, j, slot):
                mask = self.maskF if d == 0 else self.maskB
                groups = range(NT) if d == 0 else range(NT - 1, -1, -1)
                for g in groups:
                    ba = self.bank()
                    gs = slice(g * 128, (g + 1) * 128)
                    self.mm(ba[:, 0:128], KD[slot][:, gs], QD[slot][:, gs], True, True, [KD[slot], QD[slot]], [ba])
                    self.tt(AMA[slot][:, g, :], ba[:, 0:128], mask[:], ALU.mult, [ba, mask], [AMA[slot]])
                    vb = VB[vb_i[0] % 2]
                    vb_i[0] += 1
                    self.tt(vb[:], VH[j][:, g, :].unsqueeze(1).to_broadcast([128, 4, 128]),
                            self.ind4.ap.unsqueeze(2).to_broadcast([128, 4, 128]), ALU.mult, [VH[j], self.ind4], [vb])
                    bp = self.bank()
                    self.mm(bp[:, 0:512], KL[slot][:, g, :], vb.ap.rearrange("p c e -> p (c e)"), True, True, [KL[slot], vb], [bp])
                    chunks = range(4) if d == 0 else range(3, -1, -1)
                    for c4 in chunks:
                        cidx = g * 4 + c4
                        cur = S32r[d][si[d] % 4]
                        nxt = S32r[d][(si[d] + 1) % 4]
                        si[d] += 1
                        self.cp(S16A[slot][:, cidx, :], cur[:], [cur], [S16A[slot]], eng="act")
                        self.stt(nxt[:], cur[:], DV[slot][:, cidx:cidx + 1], bp[:, c4 * 128:(c4 + 1) * 128], ALU.mult, ALU.add,
                                 [cur, DV[slot], bp], [nxt])

            def scan_b(d, j, slot):
                bo = self.bank()
                for g in range(NT):
                    gs = slice(g * 128, (g + 1) * 128)
                    self.mm(bo[:, gs], VH[j][:, g, :], AMA[slot][:, g, :], True, False, [VH[j], AMA[slot]], [bo])
                    for c4 in range(4):
                        cidx = g * 4 + c4
                        csl = slice(g * 128 + c4 * 32, g * 128 + c4 * 32 + 32)
                        self.mm(bo[:, csl], S16A[slot][:, cidx, :], QD[slot][:, csl], False, c4 == 3, [S16A[slot], QD[slot]], [bo])
                return bo

            for j in range(NB):
                hTb, hT_t = cfg["get_hT"](j)
                tok = slice(j * BT, (j + 1) * BT)
                if cs is not None:
                    cst, snt = cs[j % 2]
                    self.dma(cst[:], self.I["cosT"][:, tok], writes=[cst])
                    self.dma(snt[:], self.I["sinT"][:, tok], writes=[snt])
                XO = _os.environ.get("MK_X", "")
                for t_ in range(NT):
                    bank = self.bank()
                    for kc in range(KC):
                        if "nomm" in XO:
                            break
                        self.mm(bank[:, 0:256], hTb[:, kc, t_ * 128:(t_ + 1) * 128], W[:, kc, C_HI * 128:(C_AV + 1) * 128], kc == 0, kc == KC - 1,
                                [W, hT_t], [bank])
                    if "noactcp" not in XO:
                        self.cp(VH[j][:, t_, :], bank[:, 0:128], [bank], [VH[j]], eng="act")
                    if "nodvecp" not in XO:
                        self.cp(VA[j][:, t_, :], bank[:, 128:256], [bank], [VA[j]], eng="dve")
                    if cfg["nv_out"] is not None and "nodvecp" not in XO:
                        dst, st = cfg["nv_out"]
                        self.cp(st[:, t_, :], bank[:, 128:256], [bank], [st], eng="dve")
                if cfg["nv_out"] is not None and _os.environ.get("MK_X") != "nonv":
                    dst, st = cfg["nv_out"]
                    o = self.dma(dst.rearrange("(t p) d -> p t d", p=128), st[:], [st], [])
                    self.outs_pending.append(o)
                yield None
                if self.cut <= 2:
                    continue
                pa = self.proj_fm(hTb, hT_t, W, C_CA, BT)
                pg = self.proj_fm(hTb, hT_t, W, C_CG, BT)
                sg = tmpf()
                self.act(sg[:], pg[:, 0:BT], AF.Sigmoid, [pg], [sg])
                self.tt(APAD[:, 15 + j * BT:15 + (j + 1) * BT], pa[:, 0:BT], sg[:], ALU.mult, [pa, sg], [apad_t[1 + j]])
                yield None
                if self.cut <= 3:
                    continue
                pq = self.proj_fm(hTb, hT_t, W, C_HQ, BT)
                self.act(QH[j][:], pq[:, 0:BT], AF.Silu, [pq], [QH[j]])
                ph = self.proj_fm(hTb, hT_t, W, C_HG, BT)
                self.act(GH[j][:], ph[:, 0:BT], AF.Silu, [ph], [GH[j]])
                yield None
                pzb = self.proj_fm(hTb, hT_t, W, C_HFB, BT)
                self.act(SGB[j][:], pzb[:, 0:BT], AF.Sigmoid, [pzb], [SGB[j]])
                pzf = self.proj_fm(hTb, hT_t, W, C_HFF, BT)
                sgf = tmpf()
                self.act(sgf[:], pzf[:, 0:BT], AF.Sigmoid, [pzf], [sgf])
                if self.cut <= 4:
                    continue
                dir_prep(0, j, sgf[:], [sgf], 0)
                yield None
                if self.cut <= 5:
                    continue
                scan_a(0, j, 0)
                yield None
                if self.cut <= 6:
                    continue
                for (cidx, nw, dst_t, is_k) in ((C_AQ, sp["qnw"], QT[j], False), (C_AK, sp["knw"], KT[j], True)):
                    pp = self.proj_fm(hTb, hT_t, W, cidx, BT)
                    sq = tmpf()
                    self.act(sq[:], pp[:, 0:BT], AF.Square, [pp], [sq])
                    bn = self.bank()
                    self.mm(bn[:, 0:BT], self.bd64[:], sq[:], True, True, [self.bd64, sq], [bn])
                    lnv, rstd = tmpf(), tmpf()
                    self.act(lnv[:], bn[:, 0:BT], AF.Ln, [bn], [lnv], scale=1.0 / 64, bias=self.epsc[EPS_RMS][:, 0:1])
                    self.act(rstd[:], lnv[:], AF.Exp, [lnv], [rstd], scale=-0.5)
                    need_f32 = cfg["rope"] or (is_k and cfg["nk_out"] is not None)
                    if need_f32:
                        xn = tmpf()
                        self.stt(xn[:], pp[:, 0:BT], nw[:, l:l + 1], rstd[:], ALU.mult, ALU.mult, [pp, nw, rstd], [xn])
                    if cfg["rope"]:
                        br = self.bank()
                        self.mm(br[:, 0:BT], self.ropeR[:], xn[:], True, True, [self.ropeR, xn], [br])
                        t1, t2 = tmpf(), tmpf()
                        self.tt(t1[:], xn[:], cst[:], ALU.mult, [xn, cst], [t1])
                        self.tt(t2[:], br[:, 0:BT], snt[:], ALU.mult, [br, snt], [t2])
                        if is_k:
                            self.tt(dst_t[:], t1[:], t2[:], ALU.add, [t1, t2], [dst_t])
                        else:
                            self.tt(QT[j][0:64, :], t1[0:64, :], t2[0:64, :], ALU.add, [t1, t2], [QT[j]])
                            self.tt(QTb[j][64:128, :], t1[64:128, :], t2[64:128, :], ALU.add, [t1, t2], [QTb[j]])
                    elif need_f32:
                        self.cp(dst_t[:], xn[:], [xn], [dst_t], eng="act")
                    else:
                        self.stt(QT[j][0:64, :], pp[0:64, 0:BT], nw[0:64, l:l + 1], rstd[0:64, :], ALU.mult, ALU.mult, [pp, nw, rstd], [QT[j]])
                        self.stt(QTb[j][64:128, :], pp[64:128, 0:BT], nw[64:128, l:l + 1], rstd[64:128, :], ALU.mult, ALU.mult, [pp, nw, rstd], [QTb[j]])
                    if is_k and cfg["nk_out"] is not None:
                        dst, st = cfg["nk_out"]
                        bk = self.bank()
                        for t_ in range(NT):
                            self.tr(bk[:, t_ * 128:(t_ + 1) * 128], xn[:, t_ * 128:(t_ + 1) * 128], [xn], [bk])
                        self.cp(st.ap.rearrange("p t d -> p (t d)"), bk[:, 0:BT], [bk], [st], eng="dve")
                        o = self.dma(dst.rearrange("(t p) d -> p t d", p=128), st[:], [st], [])
                        self.outs_pending.append(o)
                yield None
                bo = scan_b(0, j, 0)
                self.cp(OF[j][:], bo[:, 0:BT], [bo], [OF[j]], eng="dve")
                yield None
            if self.cut <= 7:
                self.S.barrier()
                return
            if cfg["state_out"] is not None:
                o = self.dma(cfg["state_out"](0), S32r[0][si[0] % 4][:], [S32r[0][si[0] % 4]], [])
                self.outs_pending.append(o)

            for j in range(NB):
                bank = self.bank()
                rd = [apad_t[0], apad_t[1 + j]] + ([apad_t[j]] if j > 0 else []) + ([apad_t[2 + j]] if j + 1 < NB else [])
                for k in range(31):
                    self.mm(bank[:, 0:BT], DG[:, k, :], APAD[:, j * BT + k:j * BT + k + BT], k == 0, k == 30, [DG] + rd, [bank])
                dap, dts, post = cfg["sink"](0, j)
                self.act(dap, bank[:, 0:BT], AF.Identity, [bank, cfg["cb_t"]], dts, bias=cfg["cb"])
                if post:
                    post()

            if self.cut <= 8:
                self.S.barrier()
                return
            yield "AB"
            for j in range(NB - 1, -1, -1):
                dir_prep(1, j, SGB[j][:], [SGB[j]], 1)
                yield None
                scan_a(1, j, 1)
                yield None
                bo = scan_b(1, j, 1)
                osum, sq = tmpf(), tmpf()
                self.tt(osum[:], bo[:, 0:BT], OF[j][:], ALU.add, [bo, OF[j]], [osum])
                self.act(sq[:], osum[:], AF.Square, [osum], [sq])
                bn = self.bank()
                self.mm(bn[:, 0:BT], self.onesF[:], sq[:], True, True, [self.onesF, sq], [bn])
                lnv, rstd, t1 = tmpf(), tmpf(), tmpf()
                self.act(lnv[:], bn[:, 0:BT], AF.Ln, [bn], [lnv], scale=1.0 / 128, bias=self.epsc[EPS_RMS][:, 0:1])
                self.act(rstd[:], lnv[:], AF.Exp, [lnv], [rstd], scale=-0.5)
                self.stt(t1[:], osum[:], sp["hnw"][:, l:l + 1], rstd[:], ALU.mult, ALU.mult, [osum, sp["hnw"], rstd], [t1])
                dap, dts, post = cfg["sink"](1, j)
                self.tt(dap, t1[:], GH[j][:], ALU.mult, [t1, GH[j]], dts)
                if post:
                    post()
            if cfg["state_out"] is not None:
                o = self.dma(cfg["state_out"](1), S32r[1][si[1] % 4][:], [S32r[1][si[1] % 4]], [])
                self.outs_pending.append(o)

            if self.cut <= 9:
                self.S.barrier()
                return
            yield None
            keyt = []
            for t_ in range(past // 128):
                keyt.append((KTp[:, t_ * 128:(t_ + 1) * 128], VAp[:, t_, :], [KTp, VAp]))
            for jj in range(NB):
                for t_ in range(NT):
                    keyt.append((KT[jj][:, t_ * 128:(t_ + 1) * 128], VA[jj][:, t_, :], [KT[jj], VA[jj]]))
            for j in range(NB):
                banks = [self.PS[i] for i in range(8)]
                num = banks[0:2]
                den = banks[2:4]
                sb_ = banks[4:8]
                items = [(kt, c) for kt in range(len(keyt)) for c in range(2)]

                def issue_s(i):
                    kt, c = items[i]
                    kap, vap, krd = keyt[kt]
                    bs = sb_[i % 4]
                    qz = QT[j] if c == 0 else QTb[j]
                    self.mm(bs[:, 0:BT], kap, qz[:], True, True, krd + [qz], [bs])
                    pb = PB[i % 4]
                    self.act(pb[:], bs[:, 0:BT], AF.Exp, [bs], [pb], scale=0.125)
                LOOK = 2
                for i in range(min(LOOK, len(items))):
                    issue_s(i)
                for i in range(len(items)):
                    if i + LOOK < len(items):
                        issue_s(i + LOOK)
                    kt, c = items[i]
                    kap, vap, krd = keyt[kt]
                    first, last = kt == 0, kt == len(keyt) - 1
                    pb = PB[i % 4]
                    self.mm(num[c][:, 0:BT], vap, pb[:], first, last, krd + [pb], [num[c]])
                    self.mm(den[c][:, 0:BT], self.onesB[:], pb[:], first, last, [self.onesB, pb], [den[c]])
                self.bank_i = 4
                tcs = []
                for c in range(2):
                    lnd, rd_, tc_ = tmpf(), tmpf(), tmpf()
                    self.act(lnd[:], den[c][:, 0:BT], AF.Ln, [den[c]], [lnd])
                    self.act(rd_[:], lnd[:], AF.Exp, [lnd], [rd_], scale=-1.0)
                    self.tt(tc_[:], num[c][:, 0:BT], rd_[:], ALU.mult, [num[c], rd_], [tc_])
                    tcs.append(tc_)
                o_, sq = tmpf(), tmpf()
                self.stt(o_[:], tcs[1][:], self.neglam[:, l:l + 1], tcs[0][:], ALU.mult, ALU.add, [tcs[1], self.neglam, tcs[0]], [o_])
                self.act(sq[:], o_[:], AF.Square, [o_], [sq])
                bn = self.bank()
                self.mm(bn[:, 0:BT], self.onesF[:], sq[:], True, True, [self.onesF, sq], [bn])
                lnv, rstd = tmpf(), tmpf()
                self.act(lnv[:], bn[:, 0:BT], AF.Ln, [bn], [lnv], scale=1.0 / 128, bias=self.epsc[EPS_RMS][:, 0:1])
                self.act(rstd[:], lnv[:], AF.Exp, [lnv], [rstd], scale=-0.5)
                dap, dts, post = cfg["sink"](2, j)
                self.stt(dap, o_[:], self.subw2[:, l:l + 1], rstd[:], ALU.mult, ALU.mult, [o_, self.subw2, rstd], dts)
                if post:
                    post()
        if cfg.get("cache") is None:
            self.S.barrier()

    def merge_phase(self, l, ci):
        I, O, sp, S = self.I, self.O, self.sp, self.S
        with ExitStack() as es:
            mk = lambda n, s, d: self.sbt(es, n, s, d)
            X = mk("X", [128, 8, 1024], F32)
            if ci == 0:
                src, srct = (I["xp"], []) if l == 0 else (O["yp"], [self.t_yp])
            else:
                src, srct = (I["xs"], []) if l == 0 else (O["ys"], [self.t_ys])
            S.op("sp", lambda e: e.dma_start(out=X[:], in_=src.rearrange("(t p) d -> p t d", p=128)), srct, [X], dma=True)
            if ci == 1:
                for hf in range(2):
                    self.dma(self.HT[:, 4 * hf:4 * hf + 4, :], self.ag1_in[hf].rearrange("(k p) t -> p k t", p=128), [self.t_ag1_in[hf]], [self.HT])
                MX = self.MX
                for i3 in range(3):
                    def ld1(e, i3=i3):
                        rank = e.partition_id() % 4
                        srcv = self.ag2_out[i3].rearrange("(r p) t -> p r t", p=128)
                        return e.dma_start(out=MX[:, 4 * i3:4 * i3 + 4, :], in_=srcv[:, :, bass.ds(rank * 1024, 1024)])
                    S.op("sp", ld1, [self.t_ag2_out[i3]], [self.MX], dma=True)
            WB = mk("WB", [128, 12, 1024], BF16)
            for i3 in range(3):
                self.dma(WB[:, 4 * i3:4 * i3 + 4, :], I["w_branch"][l, 512 * i3:512 * (i3 + 1), :].rearrange("(k p) c -> p k c", p=128),
                         writes=[WB], q="pool")
            WO = mk("WO", [128, KC, 1024], BF16)
            for hb in range(2):
                self.dma(WO[:, :, hb * 512:(hb + 1) * 512], I["w_out"][l, :, hb * 512:(hb + 1) * 512].rearrange("(k p) c -> p k c", p=128),
                         writes=[WO], q="pool")
            G1 = mk("G1", [128, 1024], F32)
            with ExitStack() as es2:
                self.gbc(l, ci, 0, G1, es2)
            S.barrier()
            self.tt(WO[:], WO[:], G1.ap.unsqueeze(1).to_broadcast([128, KC, 1024]), ALU.mult, [WO, G1], [WO])
            MT = mk("MT", [128, KC, 1024], BF16)
            tmp = [mk("mt%d" % i, [128, 512], F32) for i in range(7)]
            ti = [0]

            def tmpf():
                t = tmp[ti[0] % 7]
                ti[0] += 1
                return t
            MX = self.MX
            lnsq = [mk("lnsq%d" % cc, [128, 512], BF16) for cc in range(4)]
            ln_mean = mk("ln_mean", [128, 512], F32)
            ln_rstd = mk("ln_rstd", [128, 512], F32)
            for hb in range(2):
                tok = slice(hb * 512, (hb + 1) * 512)
                b1, b2 = self.bank(), self.bank()
                for cc in range(4):
                    self.mm(b1[:, :], self.onesB[:], MX[:, cc, tok], cc == 0, cc == 3, [self.onesB, MX], [b1])
                sqs = []
                for cc in range(4):
                    sq = lnsq[cc]
                    self.act(sq[:], MX[:, cc, tok], AF.Square, [MX], [sq])
                    sqs.append(sq)
                for cc in range(4):
                    self.mm(b2[:, :], self.onesB[:], sqs[cc][:], cc == 0, cc == 3, [self.onesB, sqs[cc]], [b2])
                mean, msq, var, lnv, rstd = ln_mean, tmpf(), tmpf(), tmpf(), ln_rstd
                self.ts(mean[:], b1[:, :], 1.0 / 512, None, ALU.mult, None, [b1], [mean])
                self.tt(msq[:], mean[:], mean[:], ALU.mult, [mean], [msq])
                self.stt(var[:], b2[:, :], 1.0 / 512, msq[:], ALU.mult, ALU.subtract, [b2, msq], [var])
                self.act(lnv[:], var[:], AF.Ln, [var], [lnv], bias=self.epsc[EPS_LN][:, 0:1])
                self.act(rstd[:], lnv[:], AF.Exp, [lnv], [rstd], scale=-0.5)
                for cc in range(4):
                    z = tmpf()
                    self.tt(z[:], MX[:, cc, tok], mean[:], ALU.subtract, [MX, mean], [z])
                    self.tt(z[:], z[:], rstd[:], ALU.mult, [z, rstd], [z])
                    self.act(MX[:, cc, tok], z[:], AF.Silu, [z, sp["lng"], sp["lnb"]], [MX], scale=sp["lng"][:, l, cc:cc + 1], bias=sp["lnb"][:, l, cc:cc + 1])
            GS = [mk("GS%d" % i, [128, KC, 512], BF16) for i in range(3)]
            for jb in range(2):
                for br in range(3):
                    self.load_w512(GS[br], I["w_gate"][l, :, br * 1024 + jb * 512:br * 1024 + (jb + 1) * 512])
                for j4 in range(4):
                    j = jb * 4 + j4
                    for hb in range(2):
                        tok = slice(hb * 512, (hb + 1) * 512)
                        yb, gb = [], []
                        for br in range(3):
                            b = self.bank()
                            for kc in range(4):
                                self.mm(b[:, :], WB[:, 4 * br + kc, j * 128:(j + 1) * 128], MX[:, 4 * br + kc, tok], kc == 0, kc == 3, [WB, MX], [b])
                            yb.append(b)
                        for br in range(3):
                            b = self.bank()
                            for kc in range(KC):
                                self.mm(b[:, :], GS[br][:, kc, j4 * 128:(j4 + 1) * 128], self.HT[:, kc, tok], kc == 0, kc == KC - 1, [GS[br], self.HT], [b])
                            gb.append(b)
                        acc = None
                        for br in range(3):
                            sg = tmpf()
                            self.act(sg[:], gb[br][:, :], AF.Sigmoid, [gb[br]], [sg])
                            if br == 0:
                                acc = tmpf()
                                self.tt(acc[:], yb[br][:, :], sg[:], ALU.mult, [yb[br], sg], [acc])
                            else:
                                self.tt(sg[:], yb[br][:, :], sg[:], ALU.mult, [yb[br], sg], [sg])
                                if br == 1:
                                    self.tt(acc[:], acc[:], sg[:], ALU.add, [acc, sg], [acc])
                                else:
                                    self.tt(MT[:, j, tok], acc[:], sg[:], ALU.add, [acc, sg], [MT])
            for t in range(8):
                for hb in range(2):
                    b = self.bank()
                    for kc in range(KC):
                        self.mm(b[:, :], MT[:, kc, t * 128:(t + 1) * 128], WO[:, kc, hb * 512:(hb + 1) * 512], kc == 0, kc == KC - 1, [MT, WO], [b])
                    self.tt(X[:, t, hb * 512:(hb + 1) * 512], X[:, t, hb * 512:(hb + 1) * 512], b[:, :], ALU.add, [X, b], [X])
            dst, dstt = (O["yp"], self.t_yp) if ci == 0 else (O["ys"], self.t_ys)
            self.dma(dst.rearrange("(t p) d -> p t d", p=128), X[:], [X], [dstt])

    def gbc(self, l, ci, which, out_t, es):
        I = self.I
        c0 = (16 if which == 0 else 40) * 128
        slots = [self.sbt(es, "wg%d_%d" % (which, i), [128, KC, 512], BF16) for i in range(2)]
        bm = self.sbt(es, "bmg%d" % which, [128, 1024], F32)
        self.dma(bm[:], I["bmodg"][:, l, which, :], writes=[bm])
        for hb in range(2):
            sl = slots[hb]
            self.load_w512(sl, I["w_mod"][l, :, c0 + hb * 512:c0 + (hb + 1) * 512])
            bank = self.bank()
            for kc in range(KC):
                self.mm(bank[:, :], self.SCR[:, kc, ci, :], sl[:, kc, :], kc == 0, kc == KC - 1, [sl, self.SCR], [bank])
            self.tt(out_t[:, hb * 512:(hb + 1) * 512], bank[:, :], bm[:, hb * 512:(hb + 1) * 512], ALU.add, [bank, bm], [out_t])

    def ffn_phase(self, l, ci):
        I, O, S = self.I, self.O, self.S
        with ExitStack() as es:
            mk = lambda n, s, d: self.sbt(es, n, s, d)
            X = mk("X", [128, 8, 1024], F32)
            dst, dstt = (O["yp"], self.t_yp) if ci == 0 else (O["ys"], self.t_ys)
            S.op("sp", lambda e: e.dma_start(out=X[:], in_=dst.rearrange("(t p) d -> p t d", p=128)), [dstt], [X], dma=True)
            B2 = self.modT
            with ExitStack() as es1:
                modB = T(self.modT.ap[:, 24:32, :], "modB2")
                self.norm_hT_b(X, self.A2, 24, ci, es1)
            S.barrier()
            G2 = mk("G2", [128, 1024], F32)
            with ExitStack() as es2:
                self.gbc(l, ci, 1, G2, es2)
            S.barrier()
            FF = mk("FF", [128, 22, 1024], BF16)
            WI = [mk("WI%d" % i, [128, KC, 512], BF16) for i in range(2)]
            sgt = [mk("sg%d" % i, [128, 512], F32) for i in range(3)]
            si = 0
            for i in range(11):
                sl = WI[i % 2]
                self.dma(sl[:, :, 0:256], I["w_ffn_in"][l, :, 256 * i:256 * (i + 1)].rearrange("(k p) c -> p k c", p=128), writes=[sl], q="pool")
                self.dma(sl[:, :, 256:512], I["w_ffn_in"][l, :, DFF + 256 * i:DFF + 256 * (i + 1)].rearrange("(k p) c -> p k c", p=128), writes=[sl], q="pool")
                for cc in range(2):
                    for hb in range(2):
                        tok = slice(hb * 512, (hb + 1) * 512)
                        bg, bu = self.bank(), self.bank()
                        for kc in range(KC):
                            self.mm(bg[:, :], sl[:, kc, cc * 128:(cc + 1) * 128], self.HT[:, kc, tok], kc == 0, kc == KC - 1, [sl, self.HT], [bg])
                        for kc in range(KC):
                            self.mm(bu[:, :], sl[:, kc, 256 + cc * 128:256 + (cc + 1) * 128], self.HT[:, kc, tok], kc == 0, kc == KC - 1, [sl, self.HT], [bu])
                        sg = sgt[si % 3]
                        si += 1
                        self.act(sg[:], bg[:, :], AF.Silu, [bg], [sg])
                        self.tt(FF[:, 2 * i + cc, tok], sg[:], bu[:, :], ALU.mult, [sg, bu], [FF])
            WF = [mk("WF%d" % i, [128, 11, 512], BF16) for i in range(2)]
            wi = 0
            for hb in range(2):
                banks = [self.PS[i] for i in range(8)]
                for kg in range(2):
                    sl = WF[wi % 2]
                    wi += 1
                    self.dma(sl[:], I["w_ffn_out"][l, kg * 1408:(kg + 1) * 1408, hb * 512:(hb + 1) * 512].rearrange("(k p) c -> p k c", p=128),
                             writes=[sl], q="pool")
                    self.tt(sl[:], sl[:], G2.ap[:, hb * 512:(hb + 1) * 512].unsqueeze(1).to_broadcast([128, 11, 512]), ALU.mult, [sl, G2], [sl])
                    for t in range(8):
                        for k in range(11):
                            kk = kg * 11 + k
                            self.mm(banks[t][:, :], FF[:, kk, t * 128:(t + 1) * 128], sl[:, k, :], kk == 0, kk == 21, [FF, sl], [banks[t]])
                for t in range(8):
                    self.tt(X[:, t, hb * 512:(hb + 1) * 512], X[:, t, hb * 512:(hb + 1) * 512], banks[t][:, :], ALU.add, [X, banks[t]], [X])
            o = self.dma(dst.rearrange("(t p) d -> p t d", p=128), X[:], [X], [dstt])
            if l == NL - 1:
                self.outs_pending.append(o)

    def norm_hT_b(self, X, A, b0, ci, es):
        Bv = T(self.modT.ap[:, b0:b0 + KC, :], "Bv")
        modT = self.modT

        class _B:
            pass
        self.norm_hT(X, A, _ModView(modT, b0), ci, es, lambda kc, t: (self.HT[:, kc, t * 128:(t + 1) * 128], self.HT))


class _ModView(T):
    def __init__(self, base, b0):
        self.base = base
        self.b0 = b0
        self.name = "modview"
        self.excl = False

    @property
    def ap(self):
        return self.base.ap[:, self.b0:self.b0 + KC, :]

    @property
    def lastw(self):
        return self.base.lastw

    @lastw.setter
    def lastw(self, v):
        self.base.lastw = v

    @property
    def readers(self):
        return self.base.readers

    @readers.setter
    def readers(self, v):
        self.base.readers = v

    def __getitem__(self, idx):
        return self.ap[idx]


def _consts():
    identF = np.eye(128, dtype=np.float32)
    s = np.arange(128)[:, None]
    t = np.arange(128)[None, :]
    same = (s // 32) == (t // 32)
    maskF = (same & (s <= t)).astype(np.float32)
    maskB = (same & (s >= t)).astype(np.float32)
    bd64 = ((s // 64) == (t // 64)).astype(np.float32)
    R = np.zeros((128, 128), np.float32)
    for i in range(64):
        R[2 * i + 1, 2 * i] = -1.0
        R[2 * i, 2 * i + 1] = 1.0
    n = 4096
    rows = n // 64
    row = np.broadcast_to(np.arange(rows)[:, None], (rows, 64)).reshape(-1).astype(np.float32)
    col = np.broadcast_to(np.arange(64)[None, :], (rows, 64)).reshape(-1).astype(np.float32)
    half = 32
    inv = (np.float32(10000.0) ** (-np.arange(0, half, 2, dtype=np.float32) / np.float32(half))).astype(np.float32)
    ang = np.concatenate([row[:, None] * inv, col[:, None] * inv], axis=-1).astype(np.float32)
    cos, sin = np.cos(ang).astype(np.float32), np.sin(ang).astype(np.float32)
    p = np.arange(128)
    pi = (p % 64) // 2
    cosT = np.ascontiguousarray(cos[:, pi].T)
    sinT = np.ascontiguousarray(sin[:, pi].T)
    ind4 = ((np.arange(128)[:, None] // 32) == np.arange(4)[None, :]).astype(np.float32)
    return dict(identF=identF, maskF=maskF, maskB=maskB, bd64=bd64, ropeR=R, cosT=cosT, sinT=sinT, ind4=ind4)


def _fm(v, inner=None):
    v = np.asarray(v, np.float32)
    lead = v.shape[:-1]
    n = v.shape[-1] // 128
    v = v.reshape(*lead, n, 128)
    v = np.moveaxis(v, -1, 0)
    return np.ascontiguousarray(v)


def prepare_inputs(inp):
    f = lambda k: np.asarray(inp[k], np.float32)
    x_prompt, x_sample = f("x_prompt"), f("x_sample")
    w_in = f("w_in")
    consts = _consts()
    common = dict(consts)
    common["w_mod"] = f("w_mod")
    common["bmodT"] = _fm(f("b_mod"))
    bm = f("b_mod").reshape(NL, 6, D)
    common["bmodg"] = np.ascontiguousarray(np.broadcast_to(bm[None, :, [2, 5], :], (128, NL, 2, D)))
    common["n1T"] = _fm(f("norm1"))
    common["n2T"] = _fm(f("norm2"))
    def pack(h):
        cols = []
        cols.append(w_in[:, :, 128 * h:128 * (h + 1)])
        cols.append(w_in[:, :, 512 + 128 * h:512 + 128 * (h + 1)])
        base = 1024
        hq, hi, hff, hfb, hg = [w_in[:, :, base + 512 * i + 128 * h:base + 512 * i + 128 * (h + 1)] for i in range(5)]
        ab = 1024 + 2560
        aq, ak, av = [w_in[:, :, ab + 512 * i + 128 * h:ab + 512 * i + 128 * (h + 1)] for i in range(3)]
        cols += [hq, hff, hfb, hg, aq, ak, hi, av]
        return np.concatenate(cols, axis=-1)
    packs = [pack(h) for h in range(4)]
    common["w_hm"] = np.ascontiguousarray(np.stack(packs, axis=1))
    common["w_gate"] = np.ascontiguousarray(w_in[:, :, 5120:8192])
    cw = f("conv_w")
    cwT = np.transpose(cw, (2, 0, 1)).reshape(4, 128, NL, 31)
    common["cwT"] = np.ascontiguousarray(np.transpose(cwT, (1, 2, 0, 3)))
    common["cb"] = _fm(f("conv_b"))
    common["lng"] = _fm(f("conv_ln_g"))
    common["lnb"] = _fm(f("conv_ln_b"))
    common["lbT"] = _fm(f("hgrn_lb"))
    common["hnw"] = np.ascontiguousarray(f("hgrn_norm").T)
    common["qnw"] = np.ascontiguousarray(np.tile(f("q_norm"), (1, 2)).T)
    common["knw"] = np.ascontiguousarray(np.tile(f("k_norm"), (1, 2)).T)
    common["subw"] = np.ascontiguousarray(f("subln").T)
    common["lqbc"] = np.ascontiguousarray(np.broadcast_to(f("lambda_qk")[None], (128, NL, 4, 64)))
    common["w_branch"] = f("w_branch")
    common["w_out"] = f("w_out")
    common["w_ffn_in"] = f("w_ffn_in")
    common["w_ffn_out"] = f("w_ffn_out")
    cache_k, cache_v, state = f("cache_k"), f("cache_v"), f("state_hgrn")
    c, c_ctx = f("c"), f("c_ctx")
    maps = []
    for i in range(8):
        b, r = i // 4, i % 4
        m = dict(common)
        m["xp"] = np.ascontiguousarray(x_prompt[4 * i:4 * i + 4].reshape(1024, D))
        m["xs"] = np.ascontiguousarray(x_sample[b, 1024 * r:1024 * (r + 1)])
        cv = np.stack([c_ctx, c[b]], axis=0)
        m["cvecT"] = np.ascontiguousarray(np.transpose(cv.reshape(2, KC, 128), (2, 1, 0)))
        m["w_hd"] = packs[r]
        m["cwT_hd"] = np.ascontiguousarray(common["cwT"][:, :, r, :])
        m["cb_hd"] = np.ascontiguousarray(common["cb"][:, :, r])
        m["lbT_hd"] = np.ascontiguousarray(common["lbT"][:, :, :, r])
        m["ck_hd"] = np.ascontiguousarray(cache_k[b, :, :, r].reshape(NL, 256, 128))
        m["cv_hd"] = np.ascontiguousarray(cache_v[b, :, :, r])
        m["st_hd"] = np.ascontiguousarray(state[b, :, :, r])
        for nm in LITE:
            m[nm] = np.zeros((1, 1), np.float32)
        maps.append(m)
    return maps


_NC_CACHE = {}


def kernel(**inputs):
    maps = prepare_inputs(inputs)
    if "nc" not in _NC_CACHE:
        import os
        sa = os.environ.get("MK_STOP")
        _NC_CACHE["nc"] = Prog(stop_after=int(sa) if sa else None).build()
    nc = _NC_CACHE["nc"]
    res = run_bass_kernel_spmd(nc, maps, core_ids=list(range(8)))
    R = res.results
    y_prompt = np.concatenate([R[i]["yp"].reshape(4, 256, D) for i in range(8)], axis=0)
    y_sample = np.stack([np.concatenate([R[4 * b + r]["ys"] for r in range(4)], axis=0) for b in range(2)], axis=0)
    nk = np.concatenate([R[i]["nk"] for i in range(8)], axis=0).reshape(32, NL, 256, 4, 2, 64)
    nv = np.concatenate([R[i]["nv"] for i in range(8)], axis=0).reshape(32, NL, 256, 4, 128)
    ns = np.concatenate([R[i]["ns"] for i in range(8)], axis=0)
    return (y_prompt.astype(np.float32), y_sample.astype(np.float32), nk.astype(np.float32), nv.astype(np.float32), ns.astype(np.float32))
```

```python
import math
from contextlib import ExitStack

import numpy as np
import concourse.bass as bass
import concourse.mybir as mybir
from concourse.bass_utils import run_bass_kernel_spmd

F32 = mybir.dt.float32
BF16 = mybir.dt.bfloat16
AF = mybir.ActivationFunctionType
ALU = mybir.AluOpType
AX = mybir.AxisListType

D = 1024
KC = 8
NL = 2
HC = 1280
C_CA, C_CG, C_HQ, C_HFF, C_HFB, C_HG, C_AQ, C_AK, C_HI, C_AV = range(10)
DFF = 2816
EPS_RMS = 1e-6
EPS_LN = 1e-5
GROUPS = [[0, 1, 2, 3], [4, 5, 6, 7]]
import os as _os
LITE = set(_os.environ.get("MK_LITE", "").split(",")) - {""}


class T:
    __slots__ = ("ap", "name", "lastw", "readers", "excl")

    def __init__(self, ap, name="", excl=False):
        self.ap = ap
        self.name = name
        self.lastw = None
        self.readers = []
        self.excl = excl

    def __getitem__(self, idx):
        return self.ap[idx]


class Op:
    __slots__ = ("eng", "fn", "deps", "tok", "signal", "is_dma", "multi", "inc", "clock")

    def __init__(self, eng, fn, is_dma, multi, inc):
        self.eng = eng
        self.fn = fn
        self.deps = []
        self.tok = None
        self.signal = False
        self.is_dma = is_dma
        self.multi = multi
        self.inc = inc
        self.clock = None


ENGS = ("pe", "act", "dve", "pool", "sp")
EIDX = {e: i for i, e in enumerate(ENGS)}


class Sched:
    def __init__(self, nc, n_dma_sems=14):
        self.nc = nc
        self.ops = {e: [] for e in ENGS}
        self.n_dma_sems = n_dma_sems
        self.final = []
        self.pending_fence = {e: [] for e in ENGS}
        self.last = {e: None for e in ENGS}
        self.open_dmas = []

    def _dep(self, op, prod):
        if prod is None or prod is op:
            return
        op.deps.append(prod)
        prod.signal = True

    def op(self, eng, fn, reads=(), writes=(), dma=False, multi=False, inc=None):
        o = Op(eng, fn, dma, multi, inc if inc is not None else (16 if dma else 1))
        for t in reads:
            self._dep(o, t.lastw)
            if t.excl:
                for r in t.readers:
                    if r.eng != eng:
                        self._dep(o, r)
        for t in writes:
            lw = t.lastw
            if lw is not None and not (lw.eng == eng == "pe"):
                self._dep(o, lw)
            for r in t.readers:
                self._dep(o, r)
        if self.pending_fence[eng]:
            for p in self.pending_fence[eng]:
                self._dep(o, p)
            self.pending_fence[eng] = []
        for t in reads:
            t.readers.append(o)
        for t in writes:
            t.lastw = o
            t.readers = []
        self.ops[eng].append(o)
        if dma:
            if o.inc != 1:
                self.open_dmas.append(o)
        else:
            self.last[eng] = o
        return o

    def dma(self, eng, out, in_, reads=(), writes=()):
        return self.op(eng, lambda e: e.dma_start(out=out, in_=in_), reads, writes, dma=True)

    def barrier(self):
        deps = [self.last[e] for e in ENGS if self.last[e] is not None]
        deps += self.open_dmas
        self.open_dmas = []
        for e in ENGS:
            self.pending_fence[e] = self.pending_fence[e] + list(deps)

    def finish(self, op):
        op.signal = True
        self.final.append(op)

    def emit(self):
        nc = self.nc
        with ExitStack() as es:
            sems = {e: es.enter_context(nc.semaphore("s_" + e)) for e in ENGS}
            dsems = {}
            for e in ("act", "pool", "sp"):
                for i in range(self.n_dma_sems):
                    dsems[(e, i)] = es.enter_context(nc.semaphore("d_%s%d" % (e, i)))
            csem = es.enter_context(nc.semaphore("s_cc"))
            allsem = dict(dsems)
            allsem.update(sems)
            allsem["cc"] = csem
            dcount, dprev = {}, {}
            ccount = 0
            for e in ENGS:
                cnt, nd = 0, 0
                for o in self.ops[e]:
                    if o.is_dma:
                        if o.inc == 1:
                            ccount += 1
                            o.tok = ("cc", ccount)
                            continue
                        k = (e, nd % self.n_dma_sems)
                        nd += 1
                        dcount[k] = dcount.get(k, 0) + 16
                        o.tok = (k, dcount[k])
                        if k in dprev:
                            o.deps.append(dprev[k])
                        dprev[k] = o
                    elif o.signal:
                        cnt += 1
                        o.tok = (e, cnt)
            stats = {"waits": 0, "ops": 0}
            block = es.enter_context(nc.Block())

            def run(e, eng):
                seen = {}
                for o in self.ops[e]:
                    need = {}
                    for d in o.deps:
                        k, v = d.tok
                        if seen.get(k, 0) < v and need.get(k, 0) < v:
                            need[k] = v
                    items = list(need.items())
                    attach = None
                    if items and not o.multi:
                        attach = items[0]
                        items = items[1:]
                    for k, v in items:
                        eng.wait_ge(allsem[k], v)
                        stats["waits"] += 1
                    ins = o.fn(eng)
                    stats["ops"] += 1
                    if attach is not None:
                        ins._wait_ge(allsem[attach[0]], attach[1])
                    for k, v in need.items():
                        seen[k] = v
                    if o.tok is not None:
                        ins.then_inc(allsem[o.tok[0]], o.inc)
                if e == "sp":
                    for o in self.final:
                        k, v = o.tok
                        if seen.get(k, 0) < v:
                            eng.wait_ge(allsem[k], v)
                            seen[k] = v

            @block.tensor
            def _(eng):
                run("pe", eng)

            @block.scalar
            def _(eng):
                run("act", eng)

            @block.vector
            def _(eng):
                run("dve", eng)

            @block.gpsimd
            def _(eng):
                run("pool", eng)

            @block.sync
            def _(eng):
                run("sp", eng)
            self.stats = stats


class Prog:
    def __init__(self, stop_after=None, taps=()):
        self.nc = bass.Bass("TRN2", target_bir_lowering=False)
        self.S = Sched(self.nc)
        self.stop_after = stop_after
        self.taps = taps
        import os
        self.skip = set(os.environ.get("MK_SKIP", "").split(","))
        self.dbg = os.environ.get("MK_DBG") == "1"
        self.cut = int(os.environ.get("MK_CUT", "99"))
        self.units = int(os.environ.get("MK_UNITS", "99"))
        self.bank_i = 0
        self.uid = 0

    def din(self, name, shape, dt=F32):
        if name in LITE:
            shape = [1, 1]
        return self.nc.dram_tensor(name, list(shape), dt, kind="ExternalInput").ap()

    def dout(self, name, shape, dt=F32):
        return self.nc.dram_tensor(name, list(shape), dt, kind="ExternalOutput").ap()

    def dint(self, name, shape, dt):
        return self.nc.dram_tensor(name, list(shape), dt).ap()

    def sb(self, es, name, shape, dt):
        self.uid += 1
        return es.enter_context(self.nc.sbuf_tensor("%s_%d" % (name, self.uid), list(shape), dt))[:]

    def sbt(self, es, name, shape, dt):
        return T(self.sb(es, name, shape, dt), name)

    def bank(self):
        b = self.PS[self.bank_i % 8]
        self.bank_i += 1
        return b

    def mm(self, out, lhsT, rhs, start, stop, reads, writes, tile_position=None):
        kw = {}
        if tile_position is not None:
            kw["tile_position"] = tile_position
        return self.S.op("pe", lambda e: e.matmul(out, lhsT=lhsT, rhs=rhs, start=start, stop=stop, **kw), reads, writes)

    def tr(self, out, in_, reads, writes):
        ident = self.identF
        return self.S.op("pe", lambda e: e.transpose(out=out, in_=in_, identity=ident[:]), list(reads) + [ident], writes)

    def act(self, out, in_, func, reads, writes, scale=1.0, bias=0.0, accum_out=None):
        if accum_out is not None:
            return self.S.op("act", lambda e: e.activation(out=out, in_=in_, func=func, scale=scale, bias=bias, accum_out=accum_out),
                             reads, writes, multi=True)
        return self.S.op("act", lambda e: e.activation(out=out, in_=in_, func=func, scale=scale, bias=bias), reads, writes)

    def tt(self, out, in0, in1, op, reads, writes, eng="dve"):
        return self.S.op(eng, lambda e: e.tensor_tensor(out=out, in0=in0, in1=in1, op=op), reads, writes)

    def ts(self, out, in0, s1, s2, op0, op1, reads, writes, eng="dve"):
        if op1 is None:
            return self.S.op(eng, lambda e: e.tensor_scalar(out=out, in0=in0, scalar1=s1, scalar2=None, op0=op0), reads, writes)
        return self.S.op(eng, lambda e: e.tensor_scalar(out=out, in0=in0, scalar1=s1, scalar2=s2, op0=op0, op1=op1), reads, writes)

    def stt(self, out, in0, scalar, in1, op0, op1, reads, writes):
        return self.S.op("dve", lambda e: e.scalar_tensor_tensor(out=out, in0=in0, scalar=scalar, in1=in1, op0=op0, op1=op1), reads, writes)

    def cp(self, out, in_, reads, writes, eng="dve"):
        if eng == "act":
            return self.S.op("act", lambda e: e.activation(out=out, in_=in_, func=AF.Identity), reads, writes)
        return self.S.op(eng, lambda e: e.tensor_copy(out=out, in_=in_), reads, writes)

    def memset(self, ap, val, writes, eng="pool"):
        return self.S.op(eng, lambda e: e.memset(ap, val), (), writes)

    def dma(self, out, in_, reads=(), writes=(), q="sp"):
        return self.S.dma(q, out, in_, reads, writes)

    def build(self):
        nc, S = self.nc, self.S
        I = {}
        I["xp"] = self.din("xp", [1024, D])
        I["xs"] = self.din("xs", [1024, D])
        I["cvecT"] = self.din("cvecT", [128, KC, 2])
        I["w_mod"] = self.din("w_mod", [NL, D, 6 * D])
        I["bmodT"] = self.din("bmodT", [128, NL, 48])
        I["bmodg"] = self.din("bmodg", [128, NL, 2, D])
        I["n1T"] = self.din("n1T", [128, NL, KC])
        I["n2T"] = self.din("n2T", [128, NL, KC])
        I["w_hm"] = self.din("w_hm", [NL, 4, D, HC])
        I["w_hd"] = self.din("w_hd", [NL, D, HC])
        I["w_gate"] = self.din("w_gate", [NL, D, 3 * D])
        I["cwT"] = self.din("cwT", [128, NL, 4, 31])
        I["cwT_hd"] = self.din("cwT_hd", [128, NL, 31])
        I["cb"] = self.din("cb", [128, NL, 4])
        I["cb_hd"] = self.din("cb_hd", [128, NL])
        I["lng"] = self.din("lng", [128, NL, 4])
        I["lnb"] = self.din("lnb", [128, NL, 4])
        I["lbT"] = self.din("lbT", [128, NL, 2, 4])
        I["lbT_hd"] = self.din("lbT_hd", [128, NL, 2])
        I["hnw"] = self.din("hnw", [128, NL])
        I["qnw"] = self.din("qnw", [128, NL])
        I["knw"] = self.din("knw", [128, NL])
        I["subw"] = self.din("subw", [128, NL])
        I["lqbc"] = self.din("lqbc", [128, NL, 4, 64])
        I["w_branch"] = self.din("w_branch", [NL, 1536, D])
        I["w_out"] = self.din("w_out", [NL, D, D])
        I["w_ffn_in"] = self.din("w_ffn_in", [NL, D, 2 * DFF])
        I["w_ffn_out"] = self.din("w_ffn_out", [NL, DFF, D])
        I["ck_hd"] = self.din("ck_hd", [NL, 256, 128])
        I["cv_hd"] = self.din("cv_hd", [NL, 256, 128])
        I["st_hd"] = self.din("st_hd", [NL, 2, 128, 128])
        I["identF"] = self.din("identF", [128, 128])
        I["maskF"] = self.din("maskF", [128, 128])
        I["maskB"] = self.din("maskB", [128, 128])
        I["bd64"] = self.din("bd64", [128, 128])
        I["ropeR"] = self.din("ropeR", [128, 128])
        I["ind4"] = self.din("ind4", [128, 4])
        I["cosT"] = self.din("cosT", [128, 4096])
        I["sinT"] = self.din("sinT", [128, 4096])
        O = {}
        O["yp"] = self.dout("yp", [1024, D])
        O["ys"] = self.dout("ys", [1024, D])
        O["nk"] = self.dout("nk", [4, NL, 256, 512])
        O["nv"] = self.dout("nv", [4, NL, 256, 512])
        O["ns"] = self.dout("ns", [4, NL, 2, 4, 128, 128])
        self.I, self.O = I, O
        self.ag1_in = [self.dint("ag1_in%d" % i, [512, 1024], BF16) for i in range(2)]
        self.ag1_out = [self.dint("ag1_out%d" % i, [4 * 512, 1024], BF16) for i in range(2)]
        self.ag2_in = [self.dint("ag2_in%d" % i, [128, 4096], BF16) for i in range(3)]
        self.ag2_out = [self.dint("ag2_out%d" % i, [4 * 128, 4096], BF16) for i in range(3)]
        self.t_ag1_in = [T(a, "ag1_in") for a in self.ag1_in]
        self.t_ag1_out = [T(a, "ag1_out") for a in self.ag1_out]
        self.t_ag2_in = [T(a, "ag2_in") for a in self.ag2_in]
        self.t_ag2_out = [T(a, "ag2_out") for a in self.ag2_out]
        self.t_yp = T(O["yp"], "yp")
        self.t_ys = T(O["ys"], "ys")
        self.outs_pending = []
        self.taps_out = {}

        with ExitStack() as es:
            self.es = es
            self._anon = T(None, "anon")
            self.PS = [T(es.enter_context(nc.psum_tensor("ps%d" % i, [128, 512], F32))[:], "ps%d" % i, excl=True) for i in range(8)]
            sbt = lambda n, s, d: self.sbt(es, n, s, d)
            self.identF = sbt("identF", [128, 128], F32)
            self.maskF = sbt("maskF", [128, 128], F32)
            self.maskB = sbt("maskB", [128, 128], F32)
            self.bd64 = sbt("bd64", [128, 128], F32)
            self.ropeR = sbt("ropeR", [128, 128], F32)
            self.ind4 = sbt("ind4", [128, 4], F32)
            self.onesF = sbt("onesF", [128, 128], F32)
            self.onesB = sbt("onesB", [128, 128], BF16)
            self.reset = sbt("reset", [128, 512], F32)
            self.epsc = {EPS_RMS: sbt("eps_rms", [128, 1], F32), EPS_LN: sbt("eps_ln", [128, 1], F32)}
            for nm in ("identF", "maskF", "maskB", "bd64", "ropeR", "ind4"):
                t = getattr(self, nm)
                self.dma(t[:], I[nm][:], writes=[t])
            self.memset(self.onesF[:], 1.0, [self.onesF])
            self.memset(self.onesB[:], 1.0, [self.onesB])
            self.memset(self.reset[:], 1.0, [self.reset])
            rv = self.reset.ap.rearrange("p (c t) -> p c t", t=32)
            self.memset(rv[:, :, 0:1], 0.0, [self.reset])
            self.memset(self.epsc[EPS_RMS][:], EPS_RMS, [self.epsc[EPS_RMS]])
            self.memset(self.epsc[EPS_LN][:], EPS_LN, [self.epsc[EPS_LN]])
            sp = {}
            for nm, shp in (("cvecT", [128, KC, 2]), ("bmodT", [128, NL, 48]), ("n1T", [128, NL, KC]), ("n2T", [128, NL, KC]),
                            ("cwT", [128, NL, 4, 31]), ("cwT_hd", [128, NL, 31]), ("cb", [128, NL, 4]), ("cb_hd", [128, NL]),
                            ("lng", [128, NL, 4]), ("lnb", [128, NL, 4]), ("lbT", [128, NL, 2, 4]), ("lbT_hd", [128, NL, 2]),
                            ("hnw", [128, NL]), ("qnw", [128, NL]), ("knw", [128, NL]), ("subw", [128, NL]),
                            ("lqbc", [128, NL, 4, 64])):
                t = sbt(nm, shp, F32)
                self.dma(t[:], I[nm][:], writes=[t])
                sp[nm] = t
            self.sp = sp
            self.HT = sbt("HT", [128, KC, 1024], BF16)
            self.MX = sbt("MX", [128, 12, 1024], BF16)
            self.modT = sbt("modT", [128, 48, 2], F32)
            self.A1 = sbt("A1", [128, KC, 2], F32)
            self.A2 = sbt("A2", [128, KC, 2], F32)
            self.SC = sbt("SC", [128, KC, 2], BF16)
            self.SCR = sbt("SCR", [128, KC, 2, 128], BF16)
            self.neglam = sbt("neglam", [128, NL], F32)
            self.subw2 = sbt("subw2", [128, NL], F32)
            self.lbv = sbt("lbv", [128, NL, 2, 4], F32)
            self.omlv = sbt("omlv", [128, NL, 2, 4], F32)
            self.nomlv = sbt("nomlv", [128, NL, 2, 4], F32)
            self.lbv_hd = sbt("lbv_hd", [128, NL, 2], F32)
            self.omlv_hd = sbt("omlv_hd", [128, NL, 2], F32)
            self.nomlv_hd = sbt("nomlv_hd", [128, NL, 2], F32)
            self.setup_small()
            S.barrier()
            phases = []
            for l in range(NL):
                phases += [("mod", l), ("snorm", l), ("pmix", l), ("merge0", l), ("smix", l), ("ffn0", l), ("merge1", l), ("ffn1", l)]
            for pi_, (ph, l) in enumerate(phases):
                if self.stop_after is not None and pi_ >= self.stop_after:
                    break
                if ph in self.skip:
                    continue
                if ph == "mod":
                    self.mod_phase(l)
                elif ph == "snorm":
                    self.sample_norm(l)
                elif ph == "pmix":
                    self.prompt_mixers(l)
                elif ph == "merge0":
                    self.merge_phase(l, 0)
                elif ph == "ffn0":
                    self.ffn_phase(l, 0)
                elif ph == "smix":
                    self.sample_mixers(l)
                elif ph == "merge1":
                    self.merge_phase(l, 1)
                elif ph == "ffn1":
                    self.ffn_phase(l, 1)
                S.barrier()
            for o in self.outs_pending:
                S.finish(o)
            S.emit()
        return nc

    def setup_small(self):
        sp = self.sp
        with ExitStack() as es:
            t1 = self.sbt(es, "ss_t1", [128, 64], F32)
            s01 = self.sbt(es, "ss_s01", [128, 2], F32)
            e01 = self.sbt(es, "ss_e01", [128, 2], F32)
            for l in range(NL):
                lam_init = 0.8 - 0.6 * math.exp(-0.3 * l)
                for j in range(2):
                    self.tt(t1[:], sp["lqbc"][:, l, 2 * j, :], sp["lqbc"][:, l, 2 * j + 1, :], ALU.mult, [sp["lqbc"]], [t1])
                    self.S.op("dve", lambda e, j=j: e.tensor_reduce(out=s01[:, j:j + 1], in_=t1[:], axis=AX.X, op=ALU.add), [t1], [s01])
                self.act(e01[:], s01[:], AF.Exp, [s01], [e01])
                self.tt(self.neglam[:, l:l + 1], e01[:, 1:2], e01[:, 0:1], ALU.subtract, [e01], [self.neglam])
                self.ts(self.neglam[:, l:l + 1], self.neglam[:, l:l + 1], -lam_init, None, ALU.add, None, [self.neglam], [self.neglam])
                self.ts(self.subw2[:, l:l + 1], sp["subw"][:, l:l + 1], 1.0 - lam_init, None, ALU.mult, None, [sp["subw"]], [self.subw2])
            for (src, lb, oml, noml) in ((sp["lbT"], self.lbv, self.omlv, self.nomlv), (sp["lbT_hd"], self.lbv_hd, self.omlv_hd, self.nomlv_hd)):
                self.memset(lb[:, 0], 0.0, [lb], eng="dve")
                self.tt(lb[:, 1], src[:, 1], src[:, 0], ALU.subtract, [src], [lb])
                self.act(lb[:, 1], lb[:, 1], AF.Sigmoid, [lb], [lb])
                self.ts(oml[:], lb[:], -1.0, 1.0, ALU.mult, ALU.add, [lb], [oml])
                self.ts(noml[:], oml[:], -1.0, None, ALU.mult, None, [oml], [noml])
            self.act(self.SC[:], sp["cvecT"][:], AF.Silu, [sp["cvecT"]], [self.SC])
            self.cp(self.SCR[:], self.SC.ap.unsqueeze(3).to_broadcast([128, KC, 2, 128]), [self.SC], [self.SCR])

    def load_w512(self, slot, src_rows_ap, ncols=512):
        self.dma(slot[:, :, 0:ncols], src_rows_ap.rearrange("(k p) c -> p k c", p=128), writes=[slot], q="pool")

    def mod_phase(self, l):
        I, sp = self.I, self.sp
        with ExitStack() as es:
            slots = [self.sbt(es, "wm%d" % i, [128, KC, 512], BF16) for i in range(3)]
            bank = self.bank()
            for blk in range(12):
                sl = slots[blk % 3]
                self.load_w512(sl, I["w_mod"][l, :, blk * 512:(blk + 1) * 512])
                for c4 in range(4):
                    ch = blk * 4 + c4
                    for kc in range(KC):
                        self.mm(bank[:, 2 * ch:2 * ch + 2], sl[:, kc, c4 * 128:(c4 + 1) * 128], self.SC[:, kc, :], kc == 0, kc == KC - 1,
                                [sl, self.SC], [bank])
            pv = bank.ap[:, 0:96].rearrange("p (c t) -> p c t", t=2)
            self.tt(self.modT[:], pv, sp["bmodT"][:, l, :].unsqueeze(2).to_broadcast([128, 48, 2]), ALU.add, [bank, sp["bmodT"]], [self.modT])
            for (A, nT, c0) in ((self.A1, sp["n1T"], 8), (self.A2, sp["n2T"], 32)):
                self.ts(A[:], self.modT[:, c0:c0 + KC, :], 1.0, None, ALU.add, None, [self.modT], [A])
                self.tt(A[:], A[:], nT[:, l, :].unsqueeze(2).to_broadcast([128, KC, 2]), ALU.mult, [A, nT], [A])

    def norm_hT(self, X, A, B, ci, es, dst_fn):
        ss = self.sbt(es, "n_ss", [128, 8], F32)
        lnv = self.sbt(es, "n_ln", [128, 8], F32)
        rstd = self.sbt(es, "n_rstd", [128, 8], F32)
        junk = self.sbt(es, "n_junk", [128, 1024], F32)
        xs = [self.sbt(es, "n_xs%d" % i, [128, 1024], F32) for i in range(2)]
        for t in range(8):
            self.act(junk[:], X[:, t, :], AF.Square, [X], [junk, ss], accum_out=ss[:, t:t + 1])
        self.act(lnv[:], ss[:], AF.Ln, [ss], [lnv], scale=1.0 / D, bias=self.epsc[EPS_RMS][:, 0:1])
        self.act(rstd[:], lnv[:], AF.Exp, [lnv], [rstd], scale=-0.5)
        for t in range(8):
            x_ = xs[t % 2]
            self.ts(x_[:], X[:, t, :], rstd[:, t:t + 1], None, ALU.mult, None, [X, rstd], [x_])
            for half in range(2):
                bank = self.bank()
                for k4 in range(4):
                    kc = half * 4 + k4
                    self.tr(bank[:, k4 * 128:(k4 + 1) * 128], x_[:, kc * 128:(kc + 1) * 128], [x_], [bank])
                for k4 in range(4):
                    kc = half * 4 + k4
                    dap, dt_ = dst_fn(kc, t)
                    self.act(dap, bank[:, k4 * 128:(k4 + 1) * 128], AF.Identity, [bank, A, B], [dt_],
                             scale=A[:, kc, ci:ci + 1], bias=B[:, kc, ci:ci + 1])

    def allgather(self, ag_in, ag_out, t_in, t_out):
        self.S.op("pool", lambda e: e.collective_compute("AllGather", ALU.bypass, replica_groups=GROUPS,
                                                        ins=[ag_in[:, :]], outs=[ag_out[:, :]]),
                  [t_in], [t_out], dma=True, inc=1)

    def load_X(self, X, src_ap):
        self.dma(X[:], src_ap.rearrange("(t p) d -> p t d", p=128), writes=[X])

    def sample_norm(self, l):
        with ExitStack() as es:
            X = self.sbt(es, "X", [128, 8, 1024], F32)
            src = self.I["xs"] if l == 0 else self.O["ys"]
            self.S.op("sp", lambda e: e.dma_start(out=X[:], in_=src.rearrange("(t p) d -> p t d", p=128)),
                      [self.t_ys] if l else [], [X], dma=True)
            B1 = T(self.modT.ap[:, 0:KC, :], "B1")
            self.norm_hT(X, self.A1, self.modT, 1, es, lambda kc, t: (self.HT[:, kc, t * 128:(t + 1) * 128], self.HT))
            for hf in range(2):
                self.dma(self.ag1_in[hf].rearrange("(k p) t -> p k t", p=128), self.HT[:, 4 * hf:4 * hf + 4, :], [self.HT], [self.t_ag1_in[hf]])
                self.allgather(self.ag1_in[hf], self.ag1_out[hf], self.t_ag1_in[hf], self.t_ag1_out[hf])

    def prompt_mixers(self, l):
        I, O, sp = self.I, self.O, self.sp
        with ExitStack() as es:
            with ExitStack() as es1:
                X = self.sbt(es1, "X", [128, 8, 1024], F32)
                src = I["xp"] if l == 0 else O["yp"]
                self.S.op("sp", lambda e: e.dma_start(out=X[:], in_=src.rearrange("(t p) d -> p t d", p=128)),
                          [self.t_yp] if l else [], [X], dma=True)
                self.norm_hT(X, self.A1, self.modT, 0, es1, lambda kc, t: (self.HT[:, kc, t * 128:(t + 1) * 128], self.HT))
            self.S.barrier()
            WH = [self.sbt(es, "WH%d" % i, [128, KC, HC], BF16) for i in range(2)]
            ucache = [{}, {}]
            nkst = [self.sbt(es, "nkst%d" % i, [128, 2, 128], F32) for i in range(2)]
            nvst = [self.sbt(es, "nvst%d" % i, [128, 2, 128], F32) for i in range(2)]
            ui = 0
            for h in range(4):
                W = WH[h % 2]
                self.dma(W[:], I["w_hm"][l, h].rearrange("(k p) c -> p k c", p=128), writes=[W], q="pool")
                for s in range(4):
                    cfg = dict(
                        l=l, L=256, BT=256, W=W, rope=False, past=0,
                        get_hT=lambda j, s=s: (self.HT.ap[:, :, s * 256:(s + 1) * 256], self.HT),
                        cw=sp["cwT"][:, l, h, :], cw_t=sp["cwT"], cb=sp["cb"][:, l, h:h + 1], cb_t=sp["cb"],
                        lb=[self.lbv[:, l, d, h:h + 1] for d in range(2)], oml=[self.omlv[:, l, d, h:h + 1] for d in range(2)],
                        noml=[self.nomlv[:, l, d, h:h + 1] for d in range(2)], lb_t=[self.lbv, self.omlv, self.nomlv],
                        sink=lambda idx, j, s=s, h=h: (self.MX[:, 4 * idx + h, s * 256:(s + 1) * 256], [self.MX], None),
                        state_in=None,
                        state_out=lambda d, s=s, h=h: O["ns"][s, l, d, h],
                        nk_out=(O["nk"][s, l, :, h * 128:(h + 1) * 128], nkst[ui % 2]),
                        nv_out=(O["nv"][s, l, :, h * 128:(h + 1) * 128], nvst[ui % 2]),
                        ctx=None, es=es, cache=ucache[ui % 2],
                    )
                    ui += 1
                    if ui <= self.units:
                        self.seq_unit(cfg)

    def sample_mixers(self, l):
        I, O, sp = self.I, self.O, self.sp
        with ExitStack() as es:
            W = self.sbt(es, "WHs", [128, KC, HC], BF16)
            self.dma(W[:], I["w_hd"][l].rearrange("(k p) c -> p k c", p=128), writes=[W], q="pool")
            hb = [T(self.HT.ap[:, :, i * 512:(i + 1) * 512], "hb%d" % i) for i in range(2)]
            stg = [self.sbt(es, "sk%d" % i, [128, 512], BF16) for i in range(4)]
            cnt = {"hb": 0, "stg": 0}
            ag1 = self.ag1_out

            def get_hT(j):
                t = hb[cnt["hb"] % 2]
                cnt["hb"] += 1
                rr, half = j // 2, j % 2
                for hf in range(2):
                    self.dma(t.ap[:, 4 * hf:4 * hf + 4, :],
                             ag1[hf][rr * 512:(rr + 1) * 512, half * 512:(half + 1) * 512].rearrange("(k p) t -> p k t", p=128),
                             [self.t_ag1_out[hf]], [t, self.HT])
                return (t.ap, t)

            def sink(idx, j):
                t = stg[cnt["stg"] % 4]
                cnt["stg"] += 1
                dst = self.ag2_in[idx][:, j * 512:(j + 1) * 512]
                return (t[:], [t], lambda: self.dma(dst, t[:], [t], [self.t_ag2_in[idx]]))

            cfg = dict(
                l=l, L=4096, BT=512, W=W, rope=True, past=256, get_hT=get_hT,
                cw=sp["cwT_hd"][:, l, :], cw_t=sp["cwT_hd"], cb=sp["cb_hd"][:, l:l + 1], cb_t=sp["cb_hd"],
                lb=[self.lbv_hd[:, l, d:d + 1] for d in range(2)], oml=[self.omlv_hd[:, l, d:d + 1] for d in range(2)],
                noml=[self.nomlv_hd[:, l, d:d + 1] for d in range(2)], lb_t=[self.lbv_hd, self.omlv_hd, self.nomlv_hd],
                sink=sink, state_in=lambda d: I["st_hd"][l, d], state_out=None, nk_out=None, nv_out=None, alias_mx=True,
                ctx=(I["ck_hd"][l], I["cv_hd"][l]),
            )
            self.seq_unit(cfg)
            for i3 in range(3):
                self.allgather(self.ag2_in[i3], self.ag2_out[i3], self.t_ag2_in[i3], self.t_ag2_out[i3])
            if self.dbg and l == 0:
                for i3 in range(3):
                    d_ = self.dout("dbg_ag2_%d" % i3, [512, 4096], BF16)
                    o = self.dma(d_[:, :], self.ag2_out[i3][:, :], [self.t_ag2_out[i3]], [])
                    self.outs_pending.append(o)
                for hf in range(2):
                    d_ = self.dout("dbg_ag1_%d" % hf, [2048, 1024], BF16)
                    o = self.dma(d_[:, :], self.ag1_out[hf][:, :], [self.t_ag1_out[hf]], [])
                    self.outs_pending.append(o)

    def proj_fm(self, hTb, hT_t, W, c, BT):
        bank = self.bank()
        for kc in range(KC):
            self.mm(bank[:, 0:BT], W[:, kc, c * 128:(c + 1) * 128], hTb[:, kc, 0:BT], kc == 0, kc == KC - 1, [W, hT_t], [bank])
        return bank

    def seq_unit(self, cfg):
        S = self.S
        l, L, BT, W = cfg["l"], cfg["L"], cfg["BT"], cfg["W"]
        NB = L // BT
        NT = BT // 128
        NCH = BT // 32
        past = cfg["past"]
        NK = (past + L) // 128
        sp = self.sp
        with ExitStack() as es_local:
            cache = cfg.get("cache")
            es = cfg["es"] if cache is not None else es_local

            def mk(n, s_, d):
                if cache is None:
                    return self.sbt(es, n, s_, d)
                if n not in cache:
                    cache[n] = self.sbt(es, n, s_, d)
                return cache[n]
            APAD = mk("APAD", [128, L + 30], BF16)
            if cache is not None and "apad_t" in cache:
                apad_t = cache["apad_t"]
            else:
                apad_t = [T(APAD.ap, "apad_e")] + [T(APAD.ap, "apad%d" % j) for j in range(NB)]
                if cache is not None:
                    cache["apad_t"] = apad_t
            QH = [mk("QH%d" % j, [128, BT], BF16) for j in range(NB)]
            GH = [mk("GH%d" % j, [128, BT], BF16) for j in range(NB)]
            VH = [mk("VH%d" % j, [128, NT, 128], BF16) for j in range(NB)]
            if cfg.get("alias_mx"):
                mxb = self.MX.ap.rearrange("p a b -> p (a b)")
                mxf = mxb.bitcast(F32)
                SGB = [T(mxf[:, j * BT:(j + 1) * BT], "SGB%d" % j) for j in range(NB)]
                QT = [T(mxb[:, 8192 + j * BT:8192 + (j + 1) * BT], "QT%d" % j) for j in range(NB)]
            else:
                SGB = [mk("SGB%d" % j, [128, BT], F32) for j in range(NB)]
                QT = [mk("QT%d" % j, [128, BT], BF16) for j in range(NB)]
            OF = [mk("OF%d" % j, [128, BT], BF16) for j in range(NB)]
            QTb = [mk("QTb%d" % j, [128, BT], BF16) for j in range(NB)]
            for j in range(NB):
                self.memset(QT[j][64:128, :], 0.0, [QT[j]], eng="pool")
                self.memset(QTb[j][0:64, :], 0.0, [QTb[j]], eng="pool")
            VB = [mk("VB%d" % i, [128, 4, 128], BF16) for i in range(2)]
            KTp = mk("KTp", [128, max(past, 32)], BF16)
            VAp = mk("VAp", [128, max(past // 128, 1), 128], BF16)
            KT = [mk("KT%d" % j, [128, BT], BF16) for j in range(NB)]
            VA = [mk("VA%d" % j, [128, NT, 128], BF16) for j in range(NB)]
            DG = mk("DG", [128, 31, 128], BF16)
            S32r = [[mk("S32_%d_%d" % (d, i), [128, 128], F32) for i in range(4)] for d in range(2)]
            si = [0, 0]
            S16A = [mk("S16A%d" % i, [128, NCH, 128], BF16) for i in range(2)]
            AMA = [mk("AMA%d" % i, [128, NT, 128], BF16) for i in range(2)]
            NTMP = 8
            tmp = [mk("tmp%d" % i, [128, BT], F32) for i in range(NTMP)]
            ti = [0]

            def tmpf():
                t = tmp[ti[0] % NTMP]
                ti[0] += 1
                return t
            QD = [mk("QD%d" % i, [128, BT], BF16) for i in range(2)]
            KD = [mk("KD%d" % i, [128, BT], BF16) for i in range(2)]
            KL = [mk("KL%d" % i, [128, NT, 128], BF16) for i in range(2)]
            DV = [mk("DV%d" % i, [128, NCH], F32) for i in range(2)]
            AM = [mk("AM%d" % i, [128, 128], BF16) for i in range(2)]
            PB = [mk("PB%d" % i, [128, BT], BF16) for i in range(4)]
            pb_i = [0]
            vb_i = [0]
            cs = None
            if cfg["rope"]:
                cs = [(mk("cos%d" % i, [128, BT], F32), mk("sin%d" % i, [128, BT], F32)) for i in range(2)]

            self.memset(APAD[:, 0:15], 0.0, [apad_t[0]], eng="dve")
            self.memset(APAD[:, L + 15:L + 30], 0.0, [apad_t[0]], eng="dve")
            self.tt(DG[:], self.identF.ap.unsqueeze(1).to_broadcast([128, 31, 128]),
                    cfg["cw"].unsqueeze(2).to_broadcast([128, 31, 128]), ALU.mult, [self.identF, cfg["cw_t"]], [DG])

            for d in range(2):
                if cfg["state_in"] is None:
                    self.memset(S32r[d][0][:], 0.0, [S32r[d][0]], eng="dve")
                else:
                    self.dma(S32r[d][0][:], cfg["state_in"](d), writes=[S32r[d][0]])
            if cfg["ctx"] is not None:
                ck, cv = cfg["ctx"]
                ckt = mk("ckt", [128, past // 128, 128], F32)
                self.dma(ckt[:], ck.rearrange("(t p) d -> p t d", p=128), writes=[ckt])
                self.dma(VAp[:], cv.rearrange("(t p) d -> p t d", p=128), writes=[VAp], q="pool")
                bank = self.bank()
                for t_ in range(past // 128):
                    self.tr(bank[:, t_ * 128:(t_ + 1) * 128], ckt[:, t_, :], [ckt], [bank])
                self.cp(KTp[:, 0:past], bank[:, 0:past], [bank], [KTp], eng="act")

            lbt = cfg["lb_t"]

            def dir_prep(d, j, sig_ap, sig_reads, slot):
                logf, kT, pc = tmpf(), tmpf(), tmpf()
                self.act(logf[:], sig_ap, AF.Ln, sig_reads + lbt, [logf], scale=cfg["oml"][d], bias=cfg["lb"][d])
                self.ts(kT[:], sig_ap, cfg["noml"][d], cfg["oml"][d], ALU.mult, ALU.add, sig_reads + lbt, [kT])
                rs = self.reset
                self.S.op("dve", lambda e: e.tensor_tensor_scan(out=pc[:], data0=rs[:, 0:BT], data1=logf[:], initial=0.0,
                                                                op0=ALU.mult, op1=ALU.add), [rs, logf], [pc])
                pc3 = pc.ap.rearrange("p (c t) -> p c t", t=32)
                tot = pc3[:, :, 31:32]
                totb = tot.to_broadcast([128, NCH, 32])
                if d == 0:
                    b = pc
                else:
                    b = tmpf()
                    b3 = b.ap.rearrange("p (c t) -> p c t", t=32)
                    self.tt(b3, totb, pc3, ALU.subtract, [pc], [b])
                    self.tt(b[:], b[:], logf[:], ALU.add, [b, logf], [b])
                b3 = b.ap.rearrange("p (c t) -> p c t", t=32)
                eb, enb, dd = tmpf(), tmpf(), tmpf()
                self.act(eb[:], b[:], AF.Exp, [b], [eb])
                self.tt(QD[slot][:], QH[j][:], eb[:], ALU.mult, [QH[j], eb], [QD[slot]])
                self.act(enb[:], b[:], AF.Exp, [b], [enb], scale=-1.0)
                self.tt(KD[slot][:], kT[:], enb[:], ALU.mult, [kT, enb], [KD[slot]])
                dd3 = dd.ap.rearrange("p (c t) -> p c t", t=32)
                self.tt(dd3, totb, b3, ALU.subtract, [pc, b], [dd])
                self.act(dd[:], dd[:], AF.Exp, [dd], [dd])
                self.tt(dd[:], kT[:], dd[:], ALU.mult, [kT, dd], [dd])
                self.act(DV[slot].ap.unsqueeze(2), tot, AF.Exp, [pc], [DV[slot]])
                bank = self.bank()
                for t_ in range(NT):
                    self.tr(bank[:, t_ * 128:(t_ + 1) * 128], dd[:, t_ * 128:(t_ + 1) * 128], [dd], [bank])
                self.cp(KL[slot].ap.rearrange("p t d -> p (t d)"), bank[:, 0:BT], [bank], [KL[slot]], eng="act")

            def scan_a(d, j, slot):
                mask = self.maskF if d == 0 else self.maskB
                groups = range(NT) if d == 0 else range(NT - 1, -1, -1)
                for g in groups:
                    ba = self.bank()
                    gs = slice(g * 128, (g + 1) * 128)
                    self.mm(ba[:, 0:128], KD[slot][:, gs], QD[slot][:, gs], True, True, [KD[slot], QD[slot]], [ba])
                    self.tt(AMA[slot][:, g, :], ba[:, 0:128], mask[:], ALU.mult, [ba, mask], [AMA[slot]])
                    vb = VB[vb_i[0] % 2]
                    vb_i[0] += 1
                    self.tt(vb[:], VH[j][:, g, :].unsqueeze(1).to_broadcast([128, 4, 128]),
                            self.ind4.ap.unsqueeze(2).to_broadcast([128, 4, 128]), ALU.mult, [VH[j], self.ind4], [vb])
                    bp = self.bank()
                    self.mm(bp[:, 0:512], KL[slot][:, g, :], vb.ap.rearrange("p c e -> p (c e)"), True, True, [KL[slot], vb], [bp])
                    chunks = range(4) if d == 0 else range(3, -1, -1)
                    for c4 in chunks:
                        cidx = g * 4 + c4
                        cur = S32r[d][si[d] % 4]
                        nxt = S32r[d][(si[d] + 1) % 4]
                        si[d] += 1
                        self.cp(S16A[slot][:, cidx, :], cur[:], [cur], [S16A[slot]], eng="act")
                        self.stt(nxt[:], cur[:], DV[slot][:, cidx:cidx + 1], bp[:, c4 * 128:(c4 + 1) * 128], ALU.mult, ALU.add,
                                 [cur, DV[slot], bp], [nxt])

            def scan_b(d, j, slot):
                bo = self.bank()
                for g in range(NT):
                    gs = slice(g * 128, (g + 1) * 128)
                    self.mm(bo[:, gs], VH[j][:, g, :], AMA[slot][:, g, :], True, False, [VH[j], AMA[slot]], [bo])
                    for c4 in range(4):
                        cidx = g * 4 + c4
                        csl = slice(g * 128 + c4 * 32, g * 128 + c4 * 32 + 32)
                        self.mm(bo[:, csl], S16A[slot][:, cidx, :], QD[slot][:, csl], False, c4 == 3, [S16A[slot], QD[slot]], [bo])
                return bo

            for j in range(NB):
                hTb, hT_t = cfg["get_hT"](j)
                tok = slice(j * BT, (j + 1) * BT)
                if cs is not None:
                    cst, snt = cs[j % 2]
                    self.dma(cst[:], self.I["cosT"][:, tok], writes=[cst])
                    self.dma(snt[:], self.I["sinT"][:, tok], writes=[snt])
                XO = _os.environ.get("MK_X", "")
                for t_ in range(NT):
                    bank = self.bank()
                    for kc in range(KC):
                        if "nomm" in XO:
                            break
                        self.mm(bank[:, 0:256], hTb[:, kc, t_ * 128:(t_ + 1) * 128], W[:, kc, C_HI * 128:(C_AV + 1) * 128], kc == 0, kc == KC - 1,
                                [W, hT_t], [bank])
                    if "noactcp" not in XO:
                        self.cp(VH[j][:, t_, :], bank[:, 0:128], [bank], [VH[j]], eng="act")
                    if "nodvecp" not in XO:
                        self.cp(VA[j][:, t_, :], bank[:, 128:256], [bank], [VA[j]], eng="dve")
                    if cfg["nv_out"] is not None and "nodvecp" not in XO:
                        dst, st = cfg["nv_out"]
                        self.cp(st[:, t_, :], bank[:, 128:256], [bank], [st], eng="dve")
                if cfg["nv_out"] is not None and _os.environ.get("MK_X") != "nonv":
                    dst, st = cfg["nv_out"]
                    o = self.dma(dst.rearrange("(t p) d -> p t d", p=128), st[:], [st], [])
                    self.outs_pending.append(o)
                if self.cut <= 2:
                    continue
                pa = self.proj_fm(hTb, hT_t, W, C_CA, BT)
                pg = self.proj_fm(hTb, hT_t, W, C_CG, BT)
                sg = tmpf()
                self.act(sg[:], pg[:, 0:BT], AF.Sigmoid, [pg], [sg])
                self.tt(APAD[:, 15 + j * BT:15 + (j + 1) * BT], pa[:, 0:BT], sg[:], ALU.mult, [pa, sg], [apad_t[1 + j]])
                if self.cut <= 3:
                    continue
                pzb = self.proj_fm(hTb, hT_t, W, C_HFB, BT)
                self.act(SGB[j][:], pzb[:, 0:BT], AF.Sigmoid, [pzb], [SGB[j]])
                pzf = self.proj_fm(hTb, hT_t, W, C_HFF, BT)
                sgf = tmpf()
                self.act(sgf[:], pzf[:, 0:BT], AF.Sigmoid, [pzf], [sgf])
                pq = self.proj_fm(hTb, hT_t, W, C_HQ, BT)
                self.act(QH[j][:], pq[:, 0:BT], AF.Silu, [pq], [QH[j]])
                ph = self.proj_fm(hTb, hT_t, W, C_HG, BT)
                self.act(GH[j][:], ph[:, 0:BT], AF.Silu, [ph], [GH[j]])
                if self.cut <= 4:
                    continue
                dir_prep(0, j, sgf[:], [sgf], 0)
                if self.cut <= 5:
                    continue
                scan_a(0, j, 0)
                if self.cut <= 6:
                    continue
                for (cidx, nw, dst_t, is_k) in ((C_AQ, sp["qnw"], QT[j], False), (C_AK, sp["knw"], KT[j], True)):
                    pp = self.proj_fm(hTb, hT_t, W, cidx, BT)
                    sq = tmpf()
                    self.act(sq[:], pp[:, 0:BT], AF.Square, [pp], [sq])
                    bn = self.bank()
                    self.mm(bn[:, 0:BT], self.bd64[:], sq[:], True, True, [self.bd64, sq], [bn])
                    lnv, rstd = tmpf(), tmpf()
                    self.act(lnv[:], bn[:, 0:BT], AF.Ln, [bn], [lnv], scale=1.0 / 64, bias=self.epsc[EPS_RMS][:, 0:1])
                    self.act(rstd[:], lnv[:], AF.Exp, [lnv], [rstd], scale=-0.5)
                    need_f32 = cfg["rope"] or (is_k and cfg["nk_out"] is not None)
                    if need_f32:
                        xn = tmpf()
                        self.stt(xn[:], pp[:, 0:BT], nw[:, l:l + 1], rstd[:], ALU.mult, ALU.mult, [pp, nw, rstd], [xn])
                    if cfg["rope"]:
                        br = self.bank()
                        self.mm(br[:, 0:BT], self.ropeR[:], xn[:], True, True, [self.ropeR, xn], [br])
                        t1, t2 = tmpf(), tmpf()
                        self.tt(t1[:], xn[:], cst[:], ALU.mult, [xn, cst], [t1])
                        self.tt(t2[:], br[:, 0:BT], snt[:], ALU.mult, [br, snt], [t2])
                        if is_k:
                            self.tt(dst_t[:], t1[:], t2[:], ALU.add, [t1, t2], [dst_t])
                        else:
                            self.tt(QT[j][0:64, :], t1[0:64, :], t2[0:64, :], ALU.add, [t1, t2], [QT[j]])
                            self.tt(QTb[j][64:128, :], t1[64:128, :], t2[64:128, :], ALU.add, [t1, t2], [QTb[j]])
                    elif need_f32:
                        self.cp(dst_t[:], xn[:], [xn], [dst_t], eng="act")
                    else:
                        self.stt(QT[j][0:64, :], pp[0:64, 0:BT], nw[0:64, l:l + 1], rstd[0:64, :], ALU.mult, ALU.mult, [pp, nw, rstd], [QT[j]])
                        self.stt(QTb[j][64:128, :], pp[64:128, 0:BT], nw[64:128, l:l + 1], rstd[64:128, :], ALU.mult, ALU.mult, [pp, nw, rstd], [QTb[j]])
                    if is_k and cfg["nk_out"] is not None:
                        dst, st = cfg["nk_out"]
                        bk = self.bank()
                        for t_ in range(NT):
                            self.tr(bk[:, t_ * 128:(t_ + 1) * 128], xn[:, t_ * 128:(t_ + 1) * 128], [xn], [bk])
                        self.cp(st.ap.rearrange("p t d -> p (t d)"), bk[:, 0:BT], [bk], [st], eng="dve")
                        o = self.dma(dst.rearrange("(t p) d -> p t d", p=128), st[:], [st], [])
                        self.outs_pending.append(o)
                bo = scan_b(0, j, 0)
                self.cp(OF[j][:], bo[:, 0:BT], [bo], [OF[j]], eng="dve")
            if self.cut <= 7:
                self.S.barrier()
                return
            if cfg["state_out"] is not None:
                o = self.dma(cfg["state_out"](0), S32r[0][si[0] % 4][:], [S32r[0][si[0] % 4]], [])
                self.outs_pending.append(o)

            for j in range(NB):
                bank = self.bank()
                rd = [apad_t[0], apad_t[1 + j]] + ([apad_t[j]] if j > 0 else []) + ([apad_t[2 + j]] if j + 1 < NB else [])
                for k in range(31):
                    self.mm(bank[:, 0:BT], DG[:, k, :], APAD[:, j * BT + k:j * BT + k + BT], k == 0, k == 30, [DG] + rd, [bank])
                dap, dts, post = cfg["sink"](0, j)
                self.act(dap, bank[:, 0:BT], AF.Identity, [bank, cfg["cb_t"]], dts, bias=cfg["cb"])
                if post:
                    post()

            if self.cut <= 8:
                self.S.barrier()
                return
            for j in range(NB - 1, -1, -1):
                dir_prep(1, j, SGB[j][:], [SGB[j]], 1)
                scan_a(1, j, 1)
                bo = scan_b(1, j, 1)
                osum, sq = tmpf(), tmpf()
                self.tt(osum[:], bo[:, 0:BT], OF[j][:], ALU.add, [bo, OF[j]], [osum])
                self.act(sq[:], osum[:], AF.Square, [osum], [sq])
                bn = self.bank()
                self.mm(bn[:, 0:BT], self.onesF[:], sq[:], True, True, [self.onesF, sq], [bn])
                lnv, rstd, t1 = tmpf(), tmpf(), tmpf()
                self.act(lnv[:], bn[:, 0:BT], AF.Ln, [bn], [lnv], scale=1.0 / 128, bias=self.epsc[EPS_RMS][:, 0:1])
                self.act(rstd[:], lnv[:], AF.Exp, [lnv], [rstd], scale=-0.5)
                self.stt(t1[:], osum[:], sp["hnw"][:, l:l + 1], rstd[:], ALU.mult, ALU.mult, [osum, sp["hnw"], rstd], [t1])
                dap, dts, post = cfg["sink"](1, j)
                self.tt(dap, t1[:], GH[j][:], ALU.mult, [t1, GH[j]], dts)
                if post:
                    post()
            if cfg["state_out"] is not None:
                o = self.dma(cfg["state_out"](1), S32r[1][si[1] % 4][:], [S32r[1][si[1] % 4]], [])
                self.outs_pending.append(o)

            if self.cut <= 9:
                self.S.barrier()
                return
            keyt = []
            for t_ in range(past // 128):
                keyt.append((KTp[:, t_ * 128:(t_ + 1) * 128], VAp[:, t_, :], [KTp, VAp]))
            for jj in range(NB):
                for t_ in range(NT):
                    keyt.append((KT[jj][:, t_ * 128:(t_ + 1) * 128], VA[jj][:, t_, :], [KT[jj], VA[jj]]))
            for j in range(NB):
                banks = [self.PS[i] for i in range(8)]
                num = banks[0:2]
                den = banks[2:4]
                sb_ = banks[4:8]
                items = [(kt, c) for kt in range(len(keyt)) for c in range(2)]

                def issue_s(i):
                    kt, c = items[i]
                    kap, vap, krd = keyt[kt]
                    bs = sb_[i % 4]
                    qz = QT[j] if c == 0 else QTb[j]
                    self.mm(bs[:, 0:BT], kap, qz[:], True, True, krd + [qz], [bs])
                    pb = PB[i % 4]
                    self.act(pb[:], bs[:, 0:BT], AF.Exp, [bs], [pb], scale=0.125)
                LOOK = 2
                for i in range(min(LOOK, len(items))):
                    issue_s(i)
                for i in range(len(items)):
                    if i + LOOK < len(items):
                        issue_s(i + LOOK)
                    kt, c = items[i]
                    kap, vap, krd = keyt[kt]
                    first, last = kt == 0, kt == len(keyt) - 1
                    pb = PB[i % 4]
                    self.mm(num[c][:, 0:BT], vap, pb[:], first, last, krd + [pb], [num[c]])
                    self.mm(den[c][:, 0:BT], self.onesB[:], pb[:], first, last, [self.onesB, pb], [den[c]])
                self.bank_i = 4
                tcs = []
                for c in range(2):
                    lnd, rd_, tc_ = tmpf(), tmpf(), tmpf()
                    self.act(lnd[:], den[c][:, 0:BT], AF.Ln, [den[c]], [lnd])
                    self.act(rd_[:], lnd[:], AF.Exp, [lnd], [rd_], scale=-1.0)
                    self.tt(tc_[:], num[c][:, 0:BT], rd_[:], ALU.mult, [num[c], rd_], [tc_])
                    tcs.append(tc_)
                o_, sq = tmpf(), tmpf()
                self.stt(o_[:], tcs[1][:], self.neglam[:, l:l + 1], tcs[0][:], ALU.mult, ALU.add, [tcs[1], self.neglam, tcs[0]], [o_])
                self.act(sq[:], o_[:], AF.Square, [o_], [sq])
                bn = self.bank()
                self.mm(bn[:, 0:BT], self.onesF[:], sq[:], True, True, [self.onesF, sq], [bn])
                lnv, rstd = tmpf(), tmpf()
                self.act(lnv[:], bn[:, 0:BT], AF.Ln, [bn], [lnv], scale=1.0 / 128, bias=self.epsc[EPS_RMS][:, 0:1])
                self.act(rstd[:], lnv[:], AF.Exp, [lnv], [rstd], scale=-0.5)
                dap, dts, post = cfg["sink"](2, j)
                self.stt(dap, o_[:], self.subw2[:, l:l + 1], rstd[:], ALU.mult, ALU.mult, [o_, self.subw2, rstd], dts)
                if post:
                    post()
        if cfg.get("cache") is None:
            self.S.barrier()

    def merge_phase(self, l, ci):
        I, O, sp, S = self.I, self.O, self.sp, self.S
        with ExitStack() as es:
            mk = lambda n, s, d: self.sbt(es, n, s, d)
            X = mk("X", [128, 8, 1024], F32)
            if ci == 0:
                src, srct = (I["xp"], []) if l == 0 else (O["yp"], [self.t_yp])
            else:
                src, srct = (I["xs"], []) if l == 0 else (O["ys"], [self.t_ys])
            S.op("sp", lambda e: e.dma_start(out=X[:], in_=src.rearrange("(t p) d -> p t d", p=128)), srct, [X], dma=True)
            if ci == 1:
                for hf in range(2):
                    self.dma(self.HT[:, 4 * hf:4 * hf + 4, :], self.ag1_in[hf].rearrange("(k p) t -> p k t", p=128), [self.t_ag1_in[hf]], [self.HT])
                MX = self.MX
                for i3 in range(3):
                    def ld1(e, i3=i3):
                        rank = e.partition_id() % 4
                        srcv = self.ag2_out[i3].rearrange("(r p) t -> p r t", p=128)
                        return e.dma_start(out=MX[:, 4 * i3:4 * i3 + 4, :], in_=srcv[:, :, bass.ds(rank * 1024, 1024)])
                    S.op("sp", ld1, [self.t_ag2_out[i3]], [self.MX], dma=True)
            WB = mk("WB", [128, 12, 1024], BF16)
            for i3 in range(3):
                self.dma(WB[:, 4 * i3:4 * i3 + 4, :], I["w_branch"][l, 512 * i3:512 * (i3 + 1), :].rearrange("(k p) c -> p k c", p=128),
                         writes=[WB], q="pool")
            WO = mk("WO", [128, KC, 1024], BF16)
            for hb in range(2):
                self.dma(WO[:, :, hb * 512:(hb + 1) * 512], I["w_out"][l, :, hb * 512:(hb + 1) * 512].rearrange("(k p) c -> p k c", p=128),
                         writes=[WO], q="pool")
            G1 = mk("G1", [128, 1024], F32)
            with ExitStack() as es2:
                self.gbc(l, ci, 0, G1, es2)
            S.barrier()
            self.tt(WO[:], WO[:], G1.ap.unsqueeze(1).to_broadcast([128, KC, 1024]), ALU.mult, [WO, G1], [WO])
            MT = mk("MT", [128, KC, 1024], BF16)
            tmp = [mk("mt%d" % i, [128, 512], F32) for i in range(7)]
            ti = [0]

            def tmpf():
                t = tmp[ti[0] % 7]
                ti[0] += 1
                return t
            MX = self.MX
            lnsq = [mk("lnsq%d" % cc, [128, 512], BF16) for cc in range(4)]
            ln_mean = mk("ln_mean", [128, 512], F32)
            ln_rstd = mk("ln_rstd", [128, 512], F32)
            for hb in range(2):
                tok = slice(hb * 512, (hb + 1) * 512)
                b1, b2 = self.bank(), self.bank()
                for cc in range(4):
                    self.mm(b1[:, :], self.onesB[:], MX[:, cc, tok], cc == 0, cc == 3, [self.onesB, MX], [b1])
                sqs = []
                for cc in range(4):
                    sq = lnsq[cc]
                    self.act(sq[:], MX[:, cc, tok], AF.Square, [MX], [sq])
                    sqs.append(sq)
                for cc in range(4):
                    self.mm(b2[:, :], self.onesB[:], sqs[cc][:], cc == 0, cc == 3, [self.onesB, sqs[cc]], [b2])
                mean, msq, var, lnv, rstd = ln_mean, tmpf(), tmpf(), tmpf(), ln_rstd
                self.ts(mean[:], b1[:, :], 1.0 / 512, None, ALU.mult, None, [b1], [mean])
                self.tt(msq[:], mean[:], mean[:], ALU.mult, [mean], [msq])
                self.stt(var[:], b2[:, :], 1.0 / 512, msq[:], ALU.mult, ALU.subtract, [b2, msq], [var])
                self.act(lnv[:], var[:], AF.Ln, [var], [lnv], bias=self.epsc[EPS_LN][:, 0:1])
                self.act(rstd[:], lnv[:], AF.Exp, [lnv], [rstd], scale=-0.5)
                for cc in range(4):
                    z = tmpf()
                    self.tt(z[:], MX[:, cc, tok], mean[:], ALU.subtract, [MX, mean], [z])
                    self.tt(z[:], z[:], rstd[:], ALU.mult, [z, rstd], [z])
                    self.act(MX[:, cc, tok], z[:], AF.Silu, [z, sp["lng"], sp["lnb"]], [MX], scale=sp["lng"][:, l, cc:cc + 1], bias=sp["lnb"][:, l, cc:cc + 1])
            GS = [mk("GS%d" % i, [128, KC, 512], BF16) for i in range(3)]
            for jb in range(2):
                for br in range(3):
                    self.load_w512(GS[br], I["w_gate"][l, :, br * 1024 + jb * 512:br * 1024 + (jb + 1) * 512])
                for j4 in range(4):
                    j = jb * 4 + j4
                    for hb in range(2):
                        tok = slice(hb * 512, (hb + 1) * 512)
                        yb, gb = [], []
                        for br in range(3):
                            b = self.bank()
                            for kc in range(4):
                                self.mm(b[:, :], WB[:, 4 * br + kc, j * 128:(j + 1) * 128], MX[:, 4 * br + kc, tok], kc == 0, kc == 3, [WB, MX], [b])
                            yb.append(b)
                        for br in range(3):
                            b = self.bank()
                            for kc in range(KC):
                                self.mm(b[:, :], GS[br][:, kc, j4 * 128:(j4 + 1) * 128], self.HT[:, kc, tok], kc == 0, kc == KC - 1, [GS[br], self.HT], [b])
                            gb.append(b)
                        acc = None
                        for br in range(3):
                            sg = tmpf()
                            self.act(sg[:], gb[br][:, :], AF.Sigmoid, [gb[br]], [sg])
                            if br == 0:
                                acc = tmpf()
                                self.tt(acc[:], yb[br][:, :], sg[:], ALU.mult, [yb[br], sg], [acc])
                            else:
                                self.tt(sg[:], yb[br][:, :], sg[:], ALU.mult, [yb[br], sg], [sg])
                                if br == 1:
                                    self.tt(acc[:], acc[:], sg[:], ALU.add, [acc, sg], [acc])
                                else:
                                    self.tt(MT[:, j, tok], acc[:], sg[:], ALU.add, [acc, sg], [MT])
            for t in range(8):
                for hb in range(2):
                    b = self.bank()
                    for kc in range(KC):
                        self.mm(b[:, :], MT[:, kc, t * 128:(t + 1) * 128], WO[:, kc, hb * 512:(hb + 1) * 512], kc == 0, kc == KC - 1, [MT, WO], [b])
                    self.tt(X[:, t, hb * 512:(hb + 1) * 512], X[:, t, hb * 512:(hb + 1) * 512], b[:, :], ALU.add, [X, b], [X])
            dst, dstt = (O["yp"], self.t_yp) if ci == 0 else (O["ys"], self.t_ys)
            self.dma(dst.rearrange("(t p) d -> p t d", p=128), X[:], [X], [dstt])

    def gbc(self, l, ci, which, out_t, es):
        I = self.I
        c0 = (16 if which == 0 else 40) * 128
        slots = [self.sbt(es, "wg%d_%d" % (which, i), [128, KC, 512], BF16) for i in range(2)]
        bm = self.sbt(es, "bmg%d" % which, [128, 1024], F32)
        self.dma(bm[:], I["bmodg"][:, l, which, :], writes=[bm])
        for hb in range(2):
            sl = slots[hb]
            self.load_w512(sl, I["w_mod"][l, :, c0 + hb * 512:c0 + (hb + 1) * 512])
            bank = self.bank()
            for kc in range(KC):
                self.mm(bank[:, :], self.SCR[:, kc, ci, :], sl[:, kc, :], kc == 0, kc == KC - 1, [sl, self.SCR], [bank])
            self.tt(out_t[:, hb * 512:(hb + 1) * 512], bank[:, :], bm[:, hb * 512:(hb + 1) * 512], ALU.add, [bank, bm], [out_t])

    def ffn_phase(self, l, ci):
        I, O, S = self.I, self.O, self.S
        with ExitStack() as es:
            mk = lambda n, s, d: self.sbt(es, n, s, d)
            X = mk("X", [128, 8, 1024], F32)
            dst, dstt = (O["yp"], self.t_yp) if ci == 0 else (O["ys"], self.t_ys)
            S.op("sp", lambda e: e.dma_start(out=X[:], in_=dst.rearrange("(t p) d -> p t d", p=128)), [dstt], [X], dma=True)
            B2 = self.modT
            with ExitStack() as es1:
                modB = T(self.modT.ap[:, 24:32, :], "modB2")
                self.norm_hT_b(X, self.A2, 24, ci, es1)
            S.barrier()
            G2 = mk("G2", [128, 1024], F32)
            with ExitStack() as es2:
                self.gbc(l, ci, 1, G2, es2)
            S.barrier()
            FF = mk("FF", [128, 22, 1024], BF16)
            WI = [mk("WI%d" % i, [128, KC, 512], BF16) for i in range(3)]
            sgt = [mk("sg%d" % i, [128, 512], F32) for i in range(3)]
            si = 0
            for i in range(11):
                sl = WI[i % 3]
                self.dma(sl[:, :, 0:256], I["w_ffn_in"][l, :, 256 * i:256 * (i + 1)].rearrange("(k p) c -> p k c", p=128), writes=[sl], q="pool")
                self.dma(sl[:, :, 256:512], I["w_ffn_in"][l, :, DFF + 256 * i:DFF + 256 * (i + 1)].rearrange("(k p) c -> p k c", p=128), writes=[sl], q="pool")
                for cc in range(2):
                    for hb in range(2):
                        tok = slice(hb * 512, (hb + 1) * 512)
                        bg, bu = self.bank(), self.bank()
                        for kc in range(KC):
                            self.mm(bg[:, :], sl[:, kc, cc * 128:(cc + 1) * 128], self.HT[:, kc, tok], kc == 0, kc == KC - 1, [sl, self.HT], [bg])
                        for kc in range(KC):
                            self.mm(bu[:, :], sl[:, kc, 256 + cc * 128:256 + (cc + 1) * 128], self.HT[:, kc, tok], kc == 0, kc == KC - 1, [sl, self.HT], [bu])
                        sg = sgt[si % 3]
                        si += 1
                        self.act(sg[:], bg[:, :], AF.Silu, [bg], [sg])
                        self.tt(FF[:, 2 * i + cc, tok], sg[:], bu[:, :], ALU.mult, [sg, bu], [FF])
            WF = [mk("WF%d" % i, [128, 11, 512], BF16) for i in range(2)]
            wi = 0
            for hb in range(2):
                banks = [self.PS[i] for i in range(8)]
                for kg in range(2):
                    sl = WF[wi % 2]
                    wi += 1
                    self.dma(sl[:], I["w_ffn_out"][l, kg * 1408:(kg + 1) * 1408, hb * 512:(hb + 1) * 512].rearrange("(k p) c -> p k c", p=128),
                             writes=[sl], q="pool")
                    self.tt(sl[:], sl[:], G2.ap[:, hb * 512:(hb + 1) * 512].unsqueeze(1).to_broadcast([128, 11, 512]), ALU.mult, [sl, G2], [sl])
                    for t in range(8):
                        for k in range(11):
                            kk = kg * 11 + k
                            self.mm(banks[t][:, :], FF[:, kk, t * 128:(t + 1) * 128], sl[:, k, :], kk == 0, kk == 21, [FF, sl], [banks[t]])
                for t in range(8):
                    self.tt(X[:, t, hb * 512:(hb + 1) * 512], X[:, t, hb * 512:(hb + 1) * 512], banks[t][:, :], ALU.add, [X, banks[t]], [X])
            o = self.dma(dst.rearrange("(t p) d -> p t d", p=128), X[:], [X], [dstt])
            if l == NL - 1:
                self.outs_pending.append(o)

    def norm_hT_b(self, X, A, b0, ci, es):
        Bv = T(self.modT.ap[:, b0:b0 + KC, :], "Bv")
        modT = self.modT

        class _B:
            pass
        self.norm_hT(X, A, _ModView(modT, b0), ci, es, lambda kc, t: (self.HT[:, kc, t * 128:(t + 1) * 128], self.HT))


class _ModView(T):
    def __init__(self, base, b0):
        self.base = base
        self.b0 = b0
        self.name = "modview"
        self.excl = False

    @property
    def ap(self):
        return self.base.ap[:, self.b0:self.b0 + KC, :]

    @property
    def lastw(self):
        return self.base.lastw

    @lastw.setter
    def lastw(self, v):
        self.base.lastw = v

    @property
    def readers(self):
        return self.base.readers

    @readers.setter
    def readers(self, v):
        self.base.readers = v

    def __getitem__(self, idx):
        return self.ap[idx]


def _consts():
    identF = np.eye(128, dtype=np.float32)
    s = np.arange(128)[:, None]
    t = np.arange(128)[None, :]
    same = (s // 32) == (t // 32)
    maskF = (same & (s <= t)).astype(np.float32)
    maskB = (same & (s >= t)).astype(np.float32)
    bd64 = ((s // 64) == (t // 64)).astype(np.float32)
    R = np.zeros((128, 128), np.float32)
    for i in range(64):
        R[2 * i + 1, 2 * i] = -1.0
        R[2 * i, 2 * i + 1] = 1.0
    n = 4096
    rows = n // 64
    row = np.broadcast_to(np.arange(rows)[:, None], (rows, 64)).reshape(-1).astype(np.float32)
    col = np.broadcast_to(np.arange(64)[None, :], (rows, 64)).reshape(-1).astype(np.float32)
    half = 32
    inv = (np.float32(10000.0) ** (-np.arange(0, half, 2, dtype=np.float32) / np.float32(half))).astype(np.float32)
    ang = np.concatenate([row[:, None] * inv, col[:, None] * inv], axis=-1).astype(np.float32)
    cos, sin = np.cos(ang).astype(np.float32), np.sin(ang).astype(np.float32)
    p = np.arange(128)
    pi = (p % 64) // 2
    cosT = np.ascontiguousarray(cos[:, pi].T)
    sinT = np.ascontiguousarray(sin[:, pi].T)
    ind4 = ((np.arange(128)[:, None] // 32) == np.arange(4)[None, :]).astype(np.float32)
    return dict(identF=identF, maskF=maskF, maskB=maskB, bd64=bd64, ropeR=R, cosT=cosT, sinT=sinT, ind4=ind4)


def _fm(v, inner=None):
    v = np.asarray(v, np.float32)
    lead = v.shape[:-1]
    n = v.shape[-1] // 128
    v = v.reshape(*lead, n, 128)
    v = np.moveaxis(v, -1, 0)
    return np.ascontiguousarray(v)


def prepare_inputs(inp):
    f = lambda k: np.asarray(inp[k], np.float32)
    x_prompt, x_sample = f("x_prompt"), f("x_sample")
    w_in = f("w_in")
    consts = _consts()
    common = dict(consts)
    common["w_mod"] = f("w_mod")
    common["bmodT"] = _fm(f("b_mod"))
    bm = f("b_mod").reshape(NL, 6, D)
    common["bmodg"] = np.ascontiguousarray(np.broadcast_to(bm[None, :, [2, 5], :], (128, NL, 2, D)))
    common["n1T"] = _fm(f("norm1"))
    common["n2T"] = _fm(f("norm2"))
    def pack(h):
        cols = []
        cols.append(w_in[:, :, 128 * h:128 * (h + 1)])
        cols.append(w_in[:, :, 512 + 128 * h:512 + 128 * (h + 1)])
        base = 1024
        hq, hi, hff, hfb, hg = [w_in[:, :, base + 512 * i + 128 * h:base + 512 * i + 128 * (h + 1)] for i in range(5)]
        ab = 1024 + 2560
        aq, ak, av = [w_in[:, :, ab + 512 * i + 128 * h:ab + 512 * i + 128 * (h + 1)] for i in range(3)]
        cols += [hq, hff, hfb, hg, aq, ak, hi, av]
        return np.concatenate(cols, axis=-1)
    packs = [pack(h) for h in range(4)]
    common["w_hm"] = np.ascontiguousarray(np.stack(packs, axis=1))
    common["w_gate"] = np.ascontiguousarray(w_in[:, :, 5120:8192])
    cw = f("conv_w")
    cwT = np.transpose(cw, (2, 0, 1)).reshape(4, 128, NL, 31)
    common["cwT"] = np.ascontiguousarray(np.transpose(cwT, (1, 2, 0, 3)))
    common["cb"] = _fm(f("conv_b"))
    common["lng"] = _fm(f("conv_ln_g"))
    common["lnb"] = _fm(f("conv_ln_b"))
    common["lbT"] = _fm(f("hgrn_lb"))
    common["hnw"] = np.ascontiguousarray(f("hgrn_norm").T)
    common["qnw"] = np.ascontiguousarray(np.tile(f("q_norm"), (1, 2)).T)
    common["knw"] = np.ascontiguousarray(np.tile(f("k_norm"), (1, 2)).T)
    common["subw"] = np.ascontiguousarray(f("subln").T)
    common["lqbc"] = np.ascontiguousarray(np.broadcast_to(f("lambda_qk")[None], (128, NL, 4, 64)))
    common["w_branch"] = f("w_branch")
    common["w_out"] = f("w_out")
    common["w_ffn_in"] = f("w_ffn_in")
    common["w_ffn_out"] = f("w_ffn_out")
    cache_k, cache_v, state = f("cache_k"), f("cache_v"), f("state_hgrn")
    c, c_ctx = f("c"), f("c_ctx")
    maps = []
    for i in range(8):
        b, r = i // 4, i % 4
        m = dict(common)
        m["xp"] = np.ascontiguousarray(x_prompt[4 * i:4 * i + 4].reshape(1024, D))
        m["xs"] = np.ascontiguousarray(x_sample[b, 1024 * r:1024 * (r + 1)])
        cv = np.stack([c_ctx, c[b]], axis=0)
        m["cvecT"] = np.ascontiguousarray(np.transpose(cv.reshape(2, KC, 128), (2, 1, 0)))
        m["w_hd"] = packs[r]
        m["cwT_hd"] = np.ascontiguousarray(common["cwT"][:, :, r, :])
        m["cb_hd"] = np.ascontiguousarray(common["cb"][:, :, r])
        m["lbT_hd"] = np.ascontiguousarray(common["lbT"][:, :, :, r])
        m["ck_hd"] = np.ascontiguousarray(cache_k[b, :, :, r].reshape(NL, 256, 128))
        m["cv_hd"] = np.ascontiguousarray(cache_v[b, :, :, r])
        m["st_hd"] = np.ascontiguousarray(state[b, :, :, r])
        for nm in LITE:
            m[nm] = np.zeros((1, 1), np.float32)
        maps.append(m)
    return maps


_NC_CACHE = {}


def kernel(**inputs):
    maps = prepare_inputs(inputs)
    if "nc" not in _NC_CACHE:
        import os
        sa = os.environ.get("MK_STOP")
        _NC_CACHE["nc"] = Prog(stop_after=int(sa) if sa else None).build()
    nc = _NC_CACHE["nc"]
    res = run_bass_kernel_spmd(nc, maps, core_ids=list(range(8)))
    R = res.results
    y_prompt = np.concatenate([R[i]["yp"].reshape(4, 256, D) for i in range(8)], axis=0)
    y_sample = np.stack([np.concatenate([R[4 * b + r]["ys"] for r in range(4)], axis=0) for b in range(2)], axis=0)
    nk = np.concatenate([R[i]["nk"] for i in range(8)], axis=0).reshape(32, NL, 256, 4, 2, 64)
    nv = np.concatenate([R[i]["nv"] for i in range(8)], axis=0).reshape(32, NL, 256, 4, 128)
    ns = np.concatenate([R[i]["ns"] for i in range(8)], axis=0)
    return (y_prompt.astype(np.float32), y_sample.astype(np.float32), nk.astype(np.float32), nv.astype(np.float32), ns.astype(np.float32))
```
